# Optimizing a Trainium2 kernel written in Bass

```python
import math
import jax, jax.numpy as jnp
from jax import lax
import numpy as np

D_MODEL = 4096
BATCH = 4
SEQ = 4096
DEPTH = 1

PLE_DIM = 256
D_FF = 11008
DIFF_HEADS = 8
DIFF_HEAD_DIM = 128
DIFF_Q_BLOCK = 128
DIL_PATTERNS = ((128, 1), (512, 4), (2048, 16))
DIL_GROUP_HEADS = 8
DIL_HEAD_DIM = 128
DIL_HALF_KEYS = 64
LN_EPS = 1e-5
SUBLN_EPS = 1e-5
NEG_INF = -1e30

DIFF_QK_W = DIFF_HEADS * 2 * DIFF_HEAD_DIM
DIFF_V_W = DIFF_HEADS * 2 * DIFF_HEAD_DIM
DIL_W = len(DIL_PATTERNS) * DIL_GROUP_HEADS * DIL_HEAD_DIM
DIL_OUT_W = DIL_GROUP_HEADS * DIL_HEAD_DIM
IN_PROJ_W = 2 * DIFF_QK_W + DIFF_V_W + 3 * DIL_W + 2 * D_MODEL

kernel_name = "hybrid_diffattn_dilated_macaron_deepnorm_layer"


def _alibi_slopes(n):
    return jnp.exp2(-8.0 * jnp.arange(1, n + 1, dtype=jnp.float32) / n)


def _layer_norm(x, g, b):
    xf = x.astype(jnp.float32)
    mu = jnp.mean(xf, axis=-1, keepdims=True)
    var = jnp.mean(jnp.square(xf - mu), axis=-1, keepdims=True)
    return ((xf - mu) * lax.rsqrt(var + LN_EPS) * g + b).astype(x.dtype)


def _swiglu(h, w_in, w_out):
    gate, up = jnp.split(h @ w_in, 2, axis=-1)
    return (jax.nn.silu(gate) * up) @ w_out


def _diff_attention(q, k, v, lam, slopes):
    B, S, H, _, dh = q.shape
    E = v.shape[-1]
    nb = S // DIFF_Q_BLOCK
    scale = dh ** -0.5
    qb = q.reshape(B, nb, DIFF_Q_BLOCK, H, 2, dh).transpose(1, 0, 2, 3, 4, 5)
    kpos = jnp.arange(S, dtype=jnp.int32)

    def block(args):
        qblk, bi = args
        s = jnp.einsum('bqhcd,bkhcd->bhcqk', qblk, k) * scale
        qpos = bi * DIFF_Q_BLOCK + jnp.arange(DIFF_Q_BLOCK, dtype=jnp.int32)
        dist = jnp.abs(qpos[:, None] - kpos[None, :]).astype(jnp.float32)
        s = s - slopes[None, :, None, None, None] * dist
        a = jax.nn.softmax(s, axis=-1)
        a = a[:, :, 0] - lam * a[:, :, 1]
        return jnp.einsum('bhqk,bkhe->bqhe', a, v)

    out = lax.map(block, (qb, jnp.arange(nb, dtype=jnp.int32)))
    return out.transpose(1, 0, 2, 3, 4).reshape(B, S, H, E)


def _dilated_window_attention(q, k, v, dil, slopes):
    B, S, H, Dh = q.shape
    L = S // dil
    W = DIL_HALF_KEYS
    nb = -(-L // W)
    Lp = nb * W
    N = B * dil

    def to_classes(t):
        return t.reshape(B, L, dil, H, Dh).transpose(0, 2, 1, 3, 4).reshape(N, L, H, Dh)

    qc = jnp.pad(to_classes(q), ((0, 0), (0, Lp - L), (0, 0), (0, 0))).reshape(N, nb, W, H, Dh)
    pad_kv = ((0, 0), (W, Lp - L + W), (0, 0), (0, 0))
    kc = jnp.pad(to_classes(k), pad_kv).reshape(N, nb + 2, W, H, Dh)
    vc = jnp.pad(to_classes(v), pad_kv).reshape(N, nb + 2, W, H, Dh)

    def band(t):
        return jnp.concatenate([t[:, :-2], t[:, 1:-1], t[:, 2:]], axis=2)

    kb, vb = band(kc), band(vc)
    s = jnp.einsum('nbqhd,nbkhd->nbhqk', qc, kb) * (Dh ** -0.5)
    rel = jnp.arange(3 * W)[None, :] - W - jnp.arange(W)[:, None]
    kidx = jnp.arange(nb)[:, None] * W + jnp.arange(3 * W)[None, :] - W
    valid = (jnp.abs(rel)[None] <= W) & ((kidx >= 0) & (kidx < L))[:, None, :]
    bias = -(slopes * dil)[:, None, None] * jnp.abs(rel).astype(jnp.float32)[None]
    s = jnp.where(valid[None, :, None], s + bias[None, None], NEG_INF)
    m = jnp.max(s, axis=-1, keepdims=True)
    e = jnp.exp(s - m)
    z = jnp.sum(e, axis=-1, keepdims=True)
    o = jnp.einsum('nbhqk,nbkhd->nbqhd', e / z, vb)
    lse = (m + jnp.log(z))[..., 0]
    o = o.reshape(B, dil, Lp, H, Dh)[:, :, :L].transpose(0, 2, 1, 3, 4).reshape(B, S, H, Dh)
    lse = lse.transpose(0, 1, 3, 2).reshape(B, dil, Lp, H)[:, :, :L].transpose(0, 2, 1, 3).reshape(B, S, H)
    return o, lse


def _hybrid_mixer(h, w_in, lam_q1, lam_k1, lam_q2, lam_k2, subln_g, w_branch_diff,
                  w_branch_dil, w_mix_out, lambda_init):
    B, S, _ = h.shape
    f32 = jnp.float32
    z = h @ w_in
    cuts = np.cumsum([DIFF_QK_W, DIFF_QK_W, DIFF_V_W, DIL_W, DIL_W, DIL_W, D_MODEL]).tolist()
    dq, dk, dv, lq, lk, lv, gate_a, gate_b = jnp.split(z, cuts, axis=-1)

    q = dq.reshape(B, S, DIFF_HEADS, 2, DIFF_HEAD_DIM).astype(f32)
    k = dk.reshape(B, S, DIFF_HEADS, 2, DIFF_HEAD_DIM).astype(f32)
    v = dv.reshape(B, S, DIFF_HEADS, 2 * DIFF_HEAD_DIM).astype(f32)
    lam = (jnp.exp(jnp.sum(lam_q1.astype(f32) * lam_k1.astype(f32)))
           - jnp.exp(jnp.sum(lam_q2.astype(f32) * lam_k2.astype(f32))) + lambda_init)
    oa = _diff_attention(q, k, v, lam, _alibi_slopes(DIFF_HEADS))
    oa = oa * lax.rsqrt(jnp.mean(jnp.square(oa), axis=-1, keepdims=True) + SUBLN_EPS)
    oa = (oa * subln_g.astype(f32) * (1.0 - lambda_init)).reshape(B, S, DIFF_V_W).astype(h.dtype)

    n_pat = len(DIL_PATTERNS)
    q = lq.reshape(B, S, n_pat, DIL_GROUP_HEADS, DIL_HEAD_DIM).astype(f32)
    k = lk.reshape(B, S, n_pat, DIL_GROUP_HEADS, DIL_HEAD_DIM).astype(f32)
    v = lv.reshape(B, S, n_pat, DIL_GROUP_HEADS, DIL_HEAD_DIM).astype(f32)
    dil_slopes = _alibi_slopes(n_pat * DIL_GROUP_HEADS).reshape(DIL_GROUP_HEADS, n_pat)
    outs, lses = [], []
    for g, (_window, dil) in enumerate(DIL_PATTERNS):
        o_g, lse_g = _dilated_window_attention(q[:, :, g], k[:, :, g], v[:, :, g], dil, dil_slopes[:, g])
        outs.append(o_g)
        lses.append(lse_g)
    wts = jax.nn.softmax(jnp.stack(lses, axis=0), axis=0)
    ob = jnp.sum(wts[..., None] * jnp.stack(outs, axis=0), axis=0)
    ob = ob.reshape(B, S, DIL_OUT_W).astype(h.dtype)

    y = jax.nn.sigmoid(gate_a) * (oa @ w_branch_diff) + jax.nn.sigmoid(gate_b) * (ob @ w_branch_dil)
    return y @ w_mix_out


def setup_inputs(seed: int = 0) -> dict:
    key = jax.random.key(seed)
    ks = jax.random.split(key, 32)
    f32 = jnp.float32
    beta = (8 * DEPTH) ** -0.25

    def nrm(k, shape, scale=1.0):
        return jax.random.normal(k, shape, f32) * scale

    col_scale = jnp.concatenate([
        jnp.ones((2 * DIFF_QK_W,), f32), jnp.full((DIFF_V_W,), beta, f32),
        jnp.ones((2 * DIL_W,), f32), jnp.full((DIL_W,), beta, f32),
        jnp.ones((2 * D_MODEL,), f32)])
    d = {}
    d["x"] = nrm(ks[0], (BATCH, SEQ, D_MODEL))
    d["p"] = nrm(ks[1], (DEPTH, BATCH, SEQ, PLE_DIM))
    d["ffn1_w_in"] = nrm(ks[2], (DEPTH, D_MODEL, 2 * D_FF), D_MODEL ** -0.5)
    d["ffn1_w_out"] = nrm(ks[3], (DEPTH, D_FF, D_MODEL), D_FF ** -0.5 * beta)
    d["ln1_g"] = 1.0 + nrm(ks[4], (DEPTH, D_MODEL), 0.02)
    d["ln1_b"] = nrm(ks[5], (DEPTH, D_MODEL), 0.02)
    d["w_in"] = nrm(ks[6], (DEPTH, D_MODEL, IN_PROJ_W), D_MODEL ** -0.5) * col_scale
    d["lam_q1"] = nrm(ks[7], (DEPTH, DIFF_HEAD_DIM), 0.1)
    d["lam_k1"] = nrm(ks[8], (DEPTH, DIFF_HEAD_DIM), 0.1)
    d["lam_q2"] = nrm(ks[9], (DEPTH, DIFF_HEAD_DIM), 0.1)
    d["lam_k2"] = nrm(ks[10], (DEPTH, DIFF_HEAD_DIM), 0.1)
    d["subln_g"] = 1.0 + nrm(ks[11], (DEPTH, 2 * DIFF_HEAD_DIM), 0.02)
    d["w_branch_diff"] = nrm(ks[12], (DEPTH, DIFF_V_W, D_MODEL), DIFF_V_W ** -0.5)
    d["w_branch_dil"] = nrm(ks[13], (DEPTH, DIL_OUT_W, D_MODEL), DIL_OUT_W ** -0.5)
    d["w_mix_out"] = nrm(ks[14], (DEPTH, D_MODEL, D_MODEL), D_MODEL ** -0.5 * beta)
    d["ln2_g"] = 1.0 + nrm(ks[15], (DEPTH, D_MODEL), 0.02)
    d["ln2_b"] = nrm(ks[16], (DEPTH, D_MODEL), 0.02)
    d["ffn2_w_in"] = nrm(ks[17], (DEPTH, D_MODEL, 2 * D_FF), D_MODEL ** -0.5)
    d["ffn2_w_out"] = nrm(ks[18], (DEPTH, D_FF, D_MODEL), D_FF ** -0.5 * beta)
    d["ln3_g"] = 1.0 + nrm(ks[19], (DEPTH, D_MODEL), 0.02)
    d["ln3_b"] = nrm(ks[20], (DEPTH, D_MODEL), 0.02)
    d["w_ple_gate"] = nrm(ks[21], (DEPTH, D_MODEL, D_MODEL), D_MODEL ** -0.5)
    d["w_ple_proj"] = nrm(ks[22], (DEPTH, PLE_DIM, D_MODEL), PLE_DIM ** -0.5 * beta)
    d["ln4_g"] = 1.0 + nrm(ks[23], (DEPTH, D_MODEL), 0.02)
    d["ln4_b"] = nrm(ks[24], (DEPTH, D_MODEL), 0.02)
    return d


def reference(x, p, ffn1_w_in, ffn1_w_out, ln1_g, ln1_b, w_in, lam_q1, lam_k1, lam_q2,
              lam_k2, subln_g, w_branch_diff, w_branch_dil, w_mix_out, ln2_g, ln2_b,
              ffn2_w_in, ffn2_w_out, ln3_g, ln3_b, w_ple_gate, w_ple_proj, ln4_g, ln4_b):
    alpha = (2 * DEPTH) ** 0.25
    h = x
    for i in range(DEPTH):
        lambda_init = 0.8 - 0.6 * math.exp(-0.3 * i)
        h = _layer_norm(alpha * h + 0.5 * _swiglu(h, ffn1_w_in[i], ffn1_w_out[i]), ln1_g[i], ln1_b[i])
        mix = _hybrid_mixer(h, w_in[i], lam_q1[i], lam_k1[i], lam_q2[i], lam_k2[i], subln_g[i],
                            w_branch_diff[i], w_branch_dil[i], w_mix_out[i], lambda_init)
        h = _layer_norm(alpha * h + mix, ln2_g[i], ln2_b[i])
        h = _layer_norm(alpha * h + 0.5 * _swiglu(h, ffn2_w_in[i], ffn2_w_out[i]), ln3_g[i], ln3_b[i])
        ple = jax.nn.sigmoid(h @ w_ple_gate[i]) * (p[i] @ w_ple_proj[i])
        h = _layer_norm(alpha * h + ple, ln4_g[i], ln4_b[i])
    return h
```

```python
import math
from contextlib import ExitStack
import numpy as np
import concourse.bass as bass
import concourse.mybir as mybir
from concourse.bass_utils import run_bass_kernel_spmd

F32 = mybir.dt.float32
BF16 = mybir.dt.bfloat16
I32 = mybir.dt.int32
AF = mybir.ActivationFunctionType
ALU = mybir.AluOpType
AX = mybir.AxisListType

NCORES = 8
TT = 512


def make_cfg(DM=4096, DFF=11008, T=2048):
    c = dict(DM=DM, DFF=DFF, T=T, PLE=256, NH=8, DH=128)
    c["QKW"] = 2048
    c["DVW"] = 2048
    c["DILW"] = 3072
    c["INW"] = 2 * 2048 + 2048 + 3 * 3072 + 2 * DM
    return c


class StopBuild(Exception):
    pass


class Buf:
    __slots__ = ("name", "w", "r")

    def __init__(self, name):
        self.name = name
        self.w = {}
        self.r = {}


class Sched:
    def __init__(self, nc, es):
        self.nc = nc
        self.es = es
        self.E = dict(pe=nc.tensor, act=nc.scalar, dve=nc.vector, pool=nc.gpsimd, sp=nc.sync)
        self.S = {}
        self.val = {}
        self.seen = {e: {} for e in self.E}
        for e in ("pe", "act", "dve", "pool"):
            self._mk(e)

    def _mk(self, key):
        self.S[key] = self.es.enter_context(self.nc.semaphore("s_" + key))
        self.val[key] = 0

    def op(self, e, fn, reads=(), writes=(), pwrites=(), dkey=None, inc=16):
        deps = {}

        def add(d):
            for k, v in d.items():
                if deps.get(k, 0) < v:
                    deps[k] = v

        for b in reads:
            add(b.w)
        for b in writes:
            add(b.w)
            add(b.r)
        for b in pwrites:
            add(b.r)
        eng = self.E[e]
        seen = self.seen[e]
        for k, v in deps.items():
            if k == e and e == "pe" and dkey is None:
                continue
            if seen.get(k, 0) >= v:
                continue
            eng.wait_ge(self.S[k], v)
            seen[k] = v
        ins = fn(eng)
        if dkey is not None:
            if dkey not in self.S:
                self._mk(dkey)
            self.val[dkey] += inc
            if inc == 1:
                ins.then_inc(self.S[dkey])
            else:
                ins.then_inc(self.S[dkey], inc)
            ev = (dkey, self.val[dkey])
        else:
            self.val[e] += 1
            ins.then_inc(self.S[e], 1)
            ev = (e, self.val[e])
        for b in reads:
            if b.r.get(ev[0], 0) < ev[1]:
                b.r[ev[0]] = ev[1]
        for b in writes:
            b.w = {ev[0]: ev[1]}
            b.r = {}
        for b in pwrites:
            if b.w.get(ev[0], 0) < ev[1]:
                b.w[ev[0]] = ev[1]
        return ev

    def barrier(self):
        for e, eng in self.E.items():
            for k, v in self.val.items():
                if v > 0 and self.seen[e].get(k, 0) < v:
                    eng.wait_ge(self.S[k], v)
                    self.seen[e][k] = v


class WSpec:
    def __init__(self, name, K, ncc):
        self.name = name
        self.K = K
        self.KC = K // 128
        self.ncc = ncc
        self.ksz = [128] * self.KC


def _groups(nchunks):
    out = []
    i = 0
    while i < len(nchunks):
        c0, w = nchunks[i]
        n = 1
        if w == 128:
            while i + n < len(nchunks) and nchunks[i + n] == (c0 + 128 * n, 128) and n < 32:
                n += 1
        out.append((i, c0, n, w))
        i += n
    return out


def build(cfg):
    DM, DFF, T = cfg["DM"], cfg["DFF"], cfg["T"]
    KD = DM // 128
    NTT = T // TT
    INW = cfg["INW"]
    alpha = 2.0 ** 0.25
    lambda_init = 0.8 - 0.6 * math.exp(-0.3 * 0)
    scale = 128.0 ** -0.5
    NF = DFF // 128
    fchunks = [(128 * i, 128) for i in range(NF)]
    NLT = 2 * NTT

    nc = bass.Bass("TRN2", target_bir_lowering=False)

    def din(name, shape, dt=F32):
        return nc.dram_tensor(name, list(shape), dt, kind="ExternalInput")

    def dscr(name, shape, dt=F32):
        return nc.dram_tensor(name, list(shape), dt)

    xT_in = din("xT", [DM, 2 * T])
    pT_in = din("pT", [256, T])
    meta_in = din("meta", [128, 16])
    lnp_in = din("lnp", [128, 8 * KD])
    lamv_in = din("lamv", [128, 4 * 128])
    subg_in = din("subg", [128, 256])
    out_T = nc.dram_tensor("outT", [DM, T], F32, kind="ExternalOutput")

    W = {}
    W["f1i"] = WSpec("f1i", DM, 2 * NF)
    W["f1o"] = WSpec("f1o", DFF, KD)
    W["win"] = WSpec("win", DM, INW // 128)
    W["wa"] = WSpec("wa", 2048, KD)
    W["wb"] = WSpec("wb", 1024, KD)
    W["wmix"] = WSpec("wmix", DM, KD)
    W["f2i"] = WSpec("f2i", DM, 2 * NF)
    W["f2o"] = WSpec("f2o", DFF, KD)
    W["wpg"] = WSpec("wpg", DM, KD)
    W["wpp"] = WSpec("wpp", 256, KD)
    worder = ["f1i", "f1o", "win", "wa", "wb", "wmix", "f2i", "f2o", "wpg", "wpp"]
    w_in_d = {}
    for n in worder:
        w_in_d[n] = din("w_" + n, [W[n].ncc, 128, W[n].KC * 128])

    RT = dscr("RT", [DM, 2 * T])
    HF = [dscr("HF%d" % i, [DM, 2 * T if i == 0 else T]) for i in range(3)]
    HB = [dscr("HB%d" % i, [DM, 2 * T if i == 0 else T], BF16) for i in range(3)]
    QKV = dscr("QKV", [INW, 2 * T], BF16)
    OAT = dscr("OAT", [2048, T], BF16)
    OBT = dscr("OBT", [1024, T], BF16)
    OZ = dscr("OZ", [T, 24 * 129])

    es = ExitStack()
    with es:
        S = Sched(nc, es)

        pstack = [ExitStack()]
        uniq = [0]

        def sbp(name, shape, dt):
            return es.enter_context(nc.sbuf_tensor("sb_" + name, list(shape), dt))

        def sb(name, shape, dt):
            uniq[0] += 1
            return pstack[0].enter_context(nc.sbuf_tensor("sb_%s_%d" % (name, uniq[0]), list(shape), dt))

        def new_phase():
            S.barrier()
            pstack[0].close()
            pstack[0] = ExitStack()

        PS = [es.enter_context(nc.psum_tensor("P%d" % i, [128, 512], F32)) for i in range(7)]
        PB = es.enter_context(nc.psum_tensor("PB", [128, 1024], BF16))
        PSb = [Buf("P%d" % i) for i in range(7)]
        PBb = Buf("PB")

        ident_f = sbp("ident_f", [128, 128], F32)
        ident_b = sbp("ident_b", [128, 128], BF16)
        ones_f = sbp("ones_f", [128, 128], F32)
        meta = sbp("meta", [128, 16], F32)
        lnp = sbp("lnp", [128, 8 * KD], F32)
        eps_t = sbp("eps_t", [128, 1], F32)
        cb = Buf("consts")
        S.op("pool", lambda g: g.memset(ident_f[:], 0.0), writes=[cb])
        S.op("pool", lambda g: g.affine_select(out=ident_f[:], in_=ident_f[:], pattern=[[-1, 128]],
                                               compare_op=ALU.not_equal, fill=1.0, base=0, channel_multiplier=1),
             writes=[cb])
        S.op("pool", lambda g: g.tensor_copy(out=ident_b[:], in_=ident_f[:]), reads=[cb], pwrites=[cb])
        S.op("pool", lambda g: g.memset(ones_f[:], 1.0), pwrites=[cb])
        S.op("pool", lambda g: g.memset(eps_t[:], 1e-5), pwrites=[cb])
        S.op("sp", lambda q: q.dma_start(out=meta[:], in_=meta_in[:, :]), pwrites=[cb], dkey="c_meta")
        S.op("sp", lambda q: q.dma_start(out=lnp[:], in_=lnp_in[:, :]), pwrites=[cb], dkey="c_lnp")

        NSLOT = 5
        M = {}
        slot_i = [0]

        def init_arena():
            M["arena"] = sb("arena", [128, NSLOT * 4096], BF16)
            M["slotb"] = [Buf("slot%d" % i) for i in range(NSLOT)]
            slot_i[0] = 0

        def load_wtile(n, cc, kc0=0, kc1=None):
            w = W[n]
            if kc1 is None:
                kc1 = w.KC
            nk = kc1 - kc0
            nsl = (nk * 128 + 4095) // 4096
            s0 = slot_i[0] % NSLOT
            if s0 + nsl > NSLOT:
                s0 = 0
            slot_i[0] = s0 + nsl
            arena = M["arena"]
            bufs = M["slotb"][s0:s0 + nsl]
            base = s0 * 4096
            dstv = arena[:, base:base + nk * 128]
            S.op("pool", lambda q: q.dma_start(out=dstv, in_=w_in_d[n].ap()[cc, :, kc0 * 128:kc1 * 128]), writes=bufs,
                 dkey="wl%d" % s0)

            def view(kc, ncols=128):
                k = kc - kc0
                return arena[:w.ksz[kc], base + k * 128: base + k * 128 + ncols]
            return view, bufs

        def mm_group(ps_ap, n, view, kc0, kc1, act, first=True, last=True):
            w = W[n]

            def fn(pe):
                ins = None
                for kc in range(kc0, kc1):
                    ins = pe.matmul(ps_ap, lhsT=view(kc), rhs=act(kc), start=(first and kc == kc0),
                                    stop=(last and kc == kc1 - 1))
                return ins
            return fn

        NR = 3
        L = {}
        cnt = dict(r=0, sq=0, z=0, res=0)
        P_SUM, P_SQ = 5, 6
        rtb = [[Buf("RT_%d_%d" % (t, d)) for d in range(KD)] for t in range(NLT)]

        def init_ln():
            L["rbuf"] = [sb("rbuf%d" % i, [128, TT], F32) for i in range(NR)]
            L["rbb"] = [Buf("rbuf%d" % i) for i in range(NR)]
            L["sqbuf"] = [sb("sqbuf%d" % i, [128, TT], F32) for i in range(2)]
            L["sqb"] = [Buf("sqbuf%d" % i) for i in range(2)]
            L["mean_t"] = sb("mean_t", [128, TT], F32)
            L["rstd_t"] = sb("rstd_t", [128, TT], F32)
            L["lntmp"] = sb("lntmp", [128, TT], F32)
            L["mrb"] = Buf("meanrstd")
            L["zin"] = [sb("zin%d" % i, [128, TT], F32) for i in range(2)]
            L["zinb"] = [Buf("zin%d" % i) for i in range(2)]
            L["hof"] = [sb("hof%d" % i, [128, TT], F32) for i in range(2)]
            L["hofb"] = [Buf("hof%d" % i) for i in range(2)]
            L["hob"] = [sb("hob%d" % i, [128, TT], BF16) for i in range(2)]
            L["hobb"] = [Buf("hob%d" % i) for i in range(2)]
            L["resid"] = [sb("resid%d" % i, [128, TT], F32) for i in range(2)]
            L["residb"] = [Buf("resid%d" % i) for i in range(2)]

        def init_actin():
            M["actT_in"] = sb("actT_in", [128, KD, TT], BF16)
            M["actT_inb"] = Buf("actT_in")

        def init_misc():
            M["stmp"] = [sb("stmp%d" % i, [128, TT], F32) for i in range(2)]
            M["stmpb"] = [Buf("stmp%d" % i) for i in range(2)]
            M["aT"] = [sb("aT%d" % i, [128, TT], BF16) for i in range(2)]
            M["aTb"] = [Buf("aT%d" % i) for i in range(2)]
            M["small"] = sb("small", [128, 16], F32)
            M["smallb"] = Buf("small")
            M["t1"] = [sb("t1_%d" % i, [128, TT], F32) for i in range(2)]
            M["t1b"] = [Buf("t1_%d" % i) for i in range(2)]
            return M["stmp"], M["stmpb"], M["aT"], M["aTb"], M["small"], M["smallb"], M["t1"], M["t1b"]

        def load_resid(src_dram, srcbufs, dc, tt):
            i = cnt["res"] % 2
            cnt["res"] += 1
            S.op("sp", lambda q: q.dma_start(out=L["resid"][i][:], in_=src_dram.ap()[dc * 128:(dc + 1) * 128, tt * TT:(tt + 1) * TT]),
                 reads=srcbufs, writes=[L["residb"][i]], dkey="resid%d" % i)
            return i

        def ln_piece(tt, dc, i):
            S.op("sp", lambda g: g.dma_start(out=RT.ap()[dc * 128:(dc + 1) * 128, tt * TT:(tt + 1) * TT], in_=L["rbuf"][i][:]),
                 reads=[L["rbb"][i]], writes=[rtb[tt][dc]], dkey="rst%d" % i)
            j = cnt["sq"] % 2
            cnt["sq"] += 1
            S.op("act", lambda a: a.activation(out=L["sqbuf"][j][:], in_=L["rbuf"][i][:], func=AF.Square), reads=[L["rbb"][i]], writes=[L["sqb"][j]])

            def fn(pe):
                pe.matmul(PS[P_SUM][:], lhsT=ones_f[:], rhs=L["rbuf"][i][:], start=(dc == 0), stop=(dc == KD - 1))
                return pe.matmul(PS[P_SQ][:], lhsT=ones_f[:], rhs=L["sqbuf"][j][:], start=(dc == 0), stop=(dc == KD - 1))
            if dc == 0:
                S.op("pe", fn, reads=[L["rbb"][i], L["sqb"][j], cb], writes=[PSb[P_SUM], PSb[P_SQ]])
            else:
                S.op("pe", fn, reads=[L["rbb"][i], L["sqb"][j], cb], pwrites=[PSb[P_SUM], PSb[P_SQ]])

        def ln_finish(tt, gi, dstF, dstFb, dstB, dstBb):
            S.op("dve", lambda v: v.tensor_scalar(L["mean_t"][:], PS[P_SUM][:], 1.0 / DM, None, ALU.mult), reads=[PSb[P_SUM]], writes=[L["mrb"]])
            S.op("dve", lambda v: v.tensor_tensor(L["lntmp"][:], L["mean_t"][:], L["mean_t"][:], ALU.mult), reads=[L["mrb"]], pwrites=[L["mrb"]])
            S.op("dve", lambda v: v.scalar_tensor_tensor(L["rstd_t"][:], PS[P_SQ][:], 1.0 / DM, L["lntmp"][:], ALU.mult, ALU.subtract),
                 reads=[PSb[P_SQ], L["mrb"]], pwrites=[L["mrb"]])
            S.op("act", lambda a: a.activation(out=L["lntmp"][:], in_=L["rstd_t"][:], func=AF.Sqrt, bias=eps_t[:], scale=1.0), reads=[L["mrb"], cb], pwrites=[L["mrb"]])
            S.op("dve", lambda v: v.reciprocal(L["rstd_t"][:], L["lntmp"][:]), reads=[L["mrb"]], pwrites=[L["mrb"]])
            for dc in range(KD):
                i = cnt["z"] % 2
                cnt["z"] += 1
                sl = (slice(dc * 128, (dc + 1) * 128), slice(tt * TT, (tt + 1) * TT))
                S.op("sp", lambda q: q.dma_start(out=L["zin"][i][:], in_=RT.ap()[sl]), reads=[rtb[tt][dc]], writes=[L["zinb"][i]], dkey="zin%d" % i)
                S.op("dve", lambda v: v.tensor_tensor(L["zin"][i][:], L["zin"][i][:], L["mean_t"][:], ALU.subtract), reads=[L["mrb"]], writes=[L["zinb"][i]])
                S.op("dve", lambda v: v.tensor_tensor(L["zin"][i][:], L["zin"][i][:], L["rstd_t"][:], ALU.mult), reads=[L["mrb"]], writes=[L["zinb"][i]])
                g_ap = lnp[:, (2 * gi) * KD + dc:(2 * gi) * KD + dc + 1]
                b_ap = lnp[:, (2 * gi + 1) * KD + dc:(2 * gi + 1) * KD + dc + 1]
                S.op("act", lambda a: a.activation(out=L["hof"][i][:], in_=L["zin"][i][:], func=AF.Identity, bias=b_ap, scale=g_ap),
                     reads=[L["zinb"][i], cb], writes=[L["hofb"][i]])
                S.op("sp", lambda g: g.dma_start(out=dstF.ap()[sl], in_=L["hof"][i][:]), reads=[L["hofb"][i]], writes=[dstFb[tt][dc]], dkey="hof%d" % i)
                if dstB is not None:
                    S.op("act", lambda a: a.activation(out=L["hob"][i][:], in_=L["hof"][i][:], func=AF.Copy), reads=[L["hofb"][i]], writes=[L["hobb"][i]])
                    S.op("sp", lambda g: g.dma_start(out=dstB.ap()[sl], in_=L["hob"][i][:]), reads=[L["hobb"][i]], writes=[dstBb[tt][dc]], dkey="hob%d" % i)

        def epilogue_resid(ps_i, src_ap_fn, srcbufs, resF, resFb, tt, dc):
            ri = load_resid(resF, [resFb[tt][dc]], dc, tt)
            i = cnt["r"] % NR
            cnt["r"] += 1
            S.op("dve", lambda v: v.scalar_tensor_tensor(L["rbuf"][i][:], L["resid"][ri][:], alpha, src_ap_fn(), ALU.mult, ALU.add),
                 reads=[L["residb"][ri]] + srcbufs, writes=[L["rbb"][i]])
            ln_piece(tt, dc, i)


        def load_act_tile(src_dram, srcbufs, col0, cast=False):
            S.op("pool" if cast else "sp", lambda q: q.dma_start(out=M["actT_in"][:], in_=src_dram.ap()[:, col0:col0 + TT].rearrange("(k p) t -> p k t", p=128)),
                 reads=srcbufs, writes=[M["actT_inb"]], dkey="actin")

        def dbufs(name):
            return [[Buf("%s_%d_%d" % (name, t, d)) for d in range(KD)] for t in range(NLT)]

        def flat(bb, tt):
            return bb[tt]

        xin_b = [[Buf("xin")] * KD for _ in range(NLT)]

        def ffn_phase(wi, wo, srcB, srcBb, resF, resFb, gi, dstF, dstFb, dstB, dstBb, ntiles, cast=False):
            wI, wO = W[wi], W[wo]
            init_arena()
            init_ln()
            init_actin()
            actT = sb("actT", [128, NF, TT], BF16)
            actTb = Buf("actT")
            sgbuf = [sb("sg%d" % i, [128, TT], F32) for i in range(2)]
            sgb = [Buf("sg%d" % i) for i in range(2)]
            it = 0
            for tt in range(ntiles):
                load_act_tile(srcB, flat(srcBb, tt), tt * TT, cast)
                for f in range(NF):
                    fw = fchunks[f][1]
                    vg, bg = load_wtile(wi, f)
                    vu, bu = load_wtile(wi, NF + f)
                    pg, pu = (0, 1) if it % 2 == 0 else (2, 3)
                    it += 1
                    S.op("pe", mm_group(PS[pg][:fw, :], wi, lambda kc, v=vg: v(kc, fw), 0, wI.KC, lambda kc: M["actT_in"][:, kc, :]),
                         reads=bg + [M["actT_inb"]], writes=[PSb[pg]])
                    S.op("pe", mm_group(PS[pu][:fw, :], wi, lambda kc, v=vu: v(kc, fw), 0, wI.KC, lambda kc: M["actT_in"][:, kc, :]),
                         reads=bu + [M["actT_inb"]], writes=[PSb[pu]])
                    j = it % 2
                    S.op("act", lambda a: a.activation(out=sgbuf[j][:fw, :], in_=PS[pg][:fw, :], func=AF.Silu), reads=[PSb[pg]], writes=[sgb[j]])
                    S.op("dve", lambda v: v.scalar_tensor_tensor(actT[:fw, f, :], sgbuf[j][:fw, :], 0.5, PS[pu][:fw, :], ALU.mult, ALU.mult),
                         reads=[sgb[j], PSb[pu]], pwrites=[actTb])
                half = wO.KC // 2
                for dc in range(KD):
                    v1, b1 = load_wtile(wo, dc, 0, half)
                    v2, b2 = load_wtile(wo, dc, half, wO.KC)
                    S.op("pe", mm_group(PS[4][:], wo, v1, 0, half, lambda kc: actT[:wO.ksz[kc], kc, :], True, False),
                         reads=b1 + [actTb], writes=[PSb[4]])
                    S.op("pe", mm_group(PS[4][:], wo, v2, half, wO.KC, lambda kc: actT[:wO.ksz[kc], kc, :], False, True),
                         reads=b2 + [actTb], pwrites=[PSb[4]])
                    epilogue_resid(4, lambda: PS[4][:], [PSb[4]], resF, resFb, tt, dc)
                ln_finish(tt, gi, dstF, dstFb, dstB, dstBb)

        HFb = [dbufs("HF%d" % i) for i in range(3)]
        HBb = [dbufs("HB%d" % i) for i in range(3)]
        outb = dbufs("out")

        def ck(k):
            if cfg.get('stop') == k:
                raise StopBuild()

        try:
            ck(0)
            ffn_phase("f1i", "f1o", xT_in, xin_b, xT_in, xin_b, 0, HF[0], HFb[0], HB[0], HBb[0], NLT, True)

            ck(1)
            new_phase()
            init_arena()
            init_actin()
            QKW, DILW = cfg["QKW"], cfg["DILW"]
            C_DQ, C_DK, C_DV = 0, QKW // 128, 2 * QKW // 128
            C_LQ = C_DV + 2048 // 128
            C_LK = C_LQ + DILW // 128
            C_LV = C_LK + DILW // 128
            C_GA = C_LV + DILW // 128
            C_GB = C_GA + KD
            NCC = C_GB + KD
            kv_ccs = list(range(C_DK, C_LQ)) + list(range(C_LK, C_GA))
            qkvb = [[Buf("qkv_%d_%d" % (cc, t)) for t in range(2 * NTT)] for cc in range(NCC)]
            ost = [sb("ost%d" % i, [128, TT], BF16) for i in range(2)]
            ostb = [Buf("ost%d" % i) for i in range(2)]
            k = 0
            for lt in range(2 * NTT):
                own = lt < NTT
                load_act_tile(HB[0], flat(HBb[0], lt), lt * TT)
                for cc in (range(NCC) if own else kv_ccs):
                    v, b = load_wtile("win", cc)
                    pi = k % 4
                    i = k % 2
                    k += 1
                    S.op("pe", mm_group(PS[pi][:], "win", v, 0, W["win"].KC, lambda kc: M["actT_in"][:, kc, :]), reads=b + [M["actT_inb"]], writes=[PSb[pi]])
                    if cc >= C_GA:
                        S.op("act", lambda a: a.activation(out=ost[i][:], in_=PS[pi][:], func=AF.Sigmoid), reads=[PSb[pi]], writes=[ostb[i]])
                    elif k % 2 == 0:
                        S.op("act", lambda a: a.activation(out=ost[i][:], in_=PS[pi][:], func=AF.Copy), reads=[PSb[pi]], writes=[ostb[i]])
                    else:
                        S.op("dve", lambda vv: vv.tensor_copy(out=ost[i][:], in_=PS[pi][:]), reads=[PSb[pi]], writes=[ostb[i]])
                    S.op("sp", lambda g: g.dma_start(out=QKV.ap()[cc * 128:(cc + 1) * 128, lt * TT:(lt + 1) * TT], in_=ost[i][:]),
                         reads=[ostb[i]], writes=[qkvb[cc][lt]], dkey="ost%d" % i)

            ck(2)
            new_phase()
            stmp, stmpb, aT, aTb, small, smallb, t1, t1b = init_misc()
            NKC = 2 * T // 128
            NQT = T // TT
            CA = T - 128
            TW = 2 * T - 128
            iota_i = sb("iota_i", [128, TW], I32)
            tabA = sb("tabA", [128, TW], F32)
            tabB = sb("tabB", [128, TW], F32)
            tabb = Buf("tabs")
            S.op("pool", lambda g: g.iota(iota_i[:], pattern=[[1, TW]], base=-CA, channel_multiplier=-1), writes=[tabb])
            S.op("pool", lambda g: g.tensor_copy(out=tabA[:], in_=iota_i[:]), reads=[tabb], pwrites=[tabb])
            S.op("act", lambda a: a.activation(out=tabB[:], in_=tabA[:], func=AF.Abs, bias=meta[:, 0:1], scale=1.0), reads=[tabb, cb], pwrites=[tabb])
            S.op("act", lambda a: a.activation(out=tabA[:], in_=tabA[:], func=AF.Abs), reads=[tabb], writes=[tabb])
            lamv = sb("lamv", [128, 512], F32)
            lamj = sb("lamj", [128, 128], F32)
            lams = sb("lams", [128, 4], F32)
            neglam = sb("neglam", [128, 1], F32)
            gs_t = sb("gs_t", [128, 256], F32)
            lamb = Buf("lam")
            S.op("sp", lambda q: q.dma_start(out=lamv[:], in_=lamv_in[:, :]), writes=[lamb], dkey="c_lam")
            S.op("sp", lambda q: q.dma_start(out=gs_t[:], in_=subg_in[:, :]), pwrites=[lamb], dkey="c_subg")
            S.op("dve", lambda v: v.memset(lams[:], 0.0), pwrites=[lamb])
            for j in range(2):
                S.op("dve", lambda v: v.tensor_tensor(lamj[:], lamv[:, (2 * j) * 128:(2 * j + 1) * 128], lamv[:, (2 * j + 1) * 128:(2 * j + 2) * 128], ALU.mult),
                     reads=[lamb], pwrites=[lamb])
                S.op("dve", lambda v: v.reduce_sum(lams[:, j:j + 1], lamj[:], AX.X), reads=[lamb], pwrites=[lamb])
            S.op("act", lambda a: a.activation(out=lams[:, 2:4], in_=lams[:, 0:2], func=AF.Exp), reads=[lamb], pwrites=[lamb])
            S.op("dve", lambda v: v.tensor_tensor(neglam[:], lams[:, 3:4], lams[:, 2:3], ALU.subtract), reads=[lamb], pwrites=[lamb])
            S.op("dve", lambda v: v.tensor_scalar(neglam[:], neglam[:], -lambda_init, None, ALU.add), reads=[lamb], pwrites=[lamb])
            S.op("dve", lambda v: v.tensor_scalar(gs_t[:], gs_t[:], 1.0 - lambda_init, None, ALU.mult), reads=[lamb], pwrites=[lamb])

            qT = [sb("qT%d" % i, [128, 2, T], BF16) for i in range(2)]
            kT = [sb("kT%d" % i, [128, 2, 2 * T], BF16) for i in range(2)]
            vT = sb("vT", [128, 2, 2 * T], BF16)
            vaug = [sb("vaug%d" % i, [128, NKC, 272], BF16) for i in range(2)]
            qTb = [Buf("qT%d" % i) for i in range(2)]
            kTb = [Buf("kT%d" % i) for i in range(2)]
            vTb = Buf("vT")
            vaugb = [Buf("vaug%d" % i) for i in range(2)]
            U = sb("U", [128, 2, 4, 256], F32)
            Ub = Buf("U")
            dtmp = sb("dtmp", [128, 256], F32)
            djunk = sb("djunk", [128, 256], F32)
            oab = sb("oab", [128, 256], BF16)
            dtb = Buf("dtmp")
            oaTs = [sb("oaTs%d" % i, [128, 2, TT], BF16) for i in range(2)]
            oaTsb = [Buf("oaTs%d" % i) for i in range(2)]
            oatb = [[Buf("oat_%d_%d" % (h, t)) for t in range(NQT)] for h in range(8)]
            for i in range(2):
                S.op("pool", lambda g: g.memset(vaug[i][:, :, 256:258], 1.0), writes=[vaugb[i]])

            def rows(cc0, n):
                return [b for cc in range(cc0, cc0 + n) for b in qkvb[cc]]

            for h in range(8):
                hb = h % 2
                slope = 2.0 ** (-8.0 * (h + 1) / 8)
                S.op("sp", lambda q: q.dma_start(out=qT[hb][:], in_=QKV.ap()[(C_DQ + 2 * h) * 128:(C_DQ + 2 * h + 2) * 128, 0:T].rearrange("(c p) t -> p c t", p=128)),
                     reads=rows(C_DQ + 2 * h, 2), writes=[qTb[hb]], dkey="qT%d" % hb)
                S.op("sp", lambda q: q.dma_start(out=kT[hb][:], in_=QKV.ap()[(C_DK + 2 * h) * 128:(C_DK + 2 * h + 2) * 128, :].rearrange("(c p) t -> p c t", p=128)),
                     reads=rows(C_DK + 2 * h, 2), writes=[kTb[hb]], dkey="kT%d" % hb)
                S.op("sp", lambda q: q.dma_start(out=vT[:], in_=QKV.ap()[(C_DV + 2 * h) * 128:(C_DV + 2 * h + 2) * 128, :].rearrange("(c p) t -> p c t", p=128)),
                     reads=rows(C_DV + 2 * h, 2), writes=[vTb], dkey="vT")
                for kc0 in range(0, NKC, 4):
                    def fn(pe):
                        ins = None
                        for kk in range(4):
                            for e in range(2):
                                ins = pe.transpose(PB[:, (kk * 2 + e) * 128:(kk * 2 + e + 1) * 128], vT[:, e, (kc0 + kk) * 128:(kc0 + kk + 1) * 128], ident_b[:])
                        return ins
                    S.op("pe", fn, reads=[vTb, cb], writes=[PBb])
                    S.op("act", lambda a: a.activation(out=vaug[hb][:, kc0:kc0 + 4, 0:256], in_=PB[:].rearrange("p (k f) -> p k f", f=256), func=AF.Copy),
                         reads=[PBb], pwrites=[vaugb[hb]])
                for qt in range(NQT):
                    for c in range(2):
                        def s_mm(kc):
                            j = kc % 2
                            S.op("pe", lambda pe: pe.matmul(PS[j][:], lhsT=kT[hb][:, c, kc * 128:(kc + 1) * 128], rhs=qT[hb][:, c, qt * TT:(qt + 1) * TT], start=True, stop=True),
                                 reads=[kTb[hb], qTb[hb]], writes=[PSb[j]])
                        s_mm(0)
                        for kc in range(NKC):
                            j = kc % 2
                            if kc + 1 < NKC:
                                s_mm(kc + 1)
                            if kc < NKC // 2:
                                tab, x0 = tabA, qt * TT - kc * 128 + CA
                            else:
                                tab, x0 = tabB, qt * TT - (kc - NKC // 2) * 128 + CA
                            S.op("dve", lambda v: v.scalar_tensor_tensor(stmp[j][:], tab[:, x0:x0 + TT], -slope / scale, PS[j][:], ALU.mult, ALU.add),
                                 reads=[PSb[j], tabb], writes=[stmpb[j]])
                            S.op("act", lambda a: a.activation(out=aT[j][:], in_=stmp[j][:], func=AF.Exp, scale=scale), reads=[stmpb[j]], writes=[aTb[j]])

                            def fn(pe):
                                ins = None
                                for qs in range(4):
                                    ins = pe.matmul(PS[2 + qs][:, 0:257], lhsT=aT[j][:, qs * 128:(qs + 1) * 128], rhs=vaug[hb][:, kc, 0:257],
                                                    start=(kc == 0), stop=(kc == NKC - 1))
                                return ins
                            if kc == 0:
                                S.op("pe", fn, reads=[aTb[j], vaugb[hb]], writes=[PSb[2], PSb[3], PSb[4], PSb[5]])
                            else:
                                S.op("pe", fn, reads=[aTb[j], vaugb[hb]], pwrites=[PSb[2], PSb[3], PSb[4], PSb[5]])
                        for qs in range(4):
                            S.op("dve", lambda v: v.reciprocal(small[:, qs:qs + 1], PS[2 + qs][:, 256:257]), reads=[PSb[2 + qs]], writes=[smallb])
                            S.op("dve", lambda v: v.tensor_scalar(U[:, c, qs, :], PS[2 + qs][:, 0:256], small[:, qs:qs + 1], None, ALU.mult),
                                 reads=[PSb[2 + qs], smallb], pwrites=[Ub])
                    ob_i = (h * NQT + qt) % 2
                    for qs in range(4):
                        S.op("dve", lambda v: v.scalar_tensor_tensor(dtmp[:], U[:, 1, qs, :], neglam[:, 0:1], U[:, 0, qs, :], ALU.mult, ALU.add),
                             reads=[Ub, lamb], writes=[dtb])
                        S.op("dve", lambda v: v.tensor_tensor(djunk[:], dtmp[:], dtmp[:], ALU.mult), reads=[dtb], pwrites=[dtb])
                        S.op("dve", lambda v: v.reduce_sum(small[:, 8:9], djunk[:], AX.X), reads=[dtb], writes=[smallb])
                        S.op("dve", lambda v: v.tensor_scalar(small[:, 9:10], small[:, 8:9], 1.0 / 256, 1e-5, ALU.mult, ALU.add), reads=[smallb], writes=[smallb])
                        S.op("act", lambda a: a.activation(out=small[:, 10:11], in_=small[:, 9:10], func=AF.Sqrt), reads=[smallb], writes=[smallb])
                        S.op("dve", lambda v: v.reciprocal(small[:, 11:12], small[:, 10:11]), reads=[smallb], writes=[smallb])
                        S.op("dve", lambda v: v.scalar_tensor_tensor(oab[:], dtmp[:], small[:, 11:12], gs_t[:], ALU.mult, ALU.mult),
                             reads=[dtb, smallb, lamb], pwrites=[dtb])

                        def fn(pe):
                            ins = None
                            for e in range(2):
                                ins = pe.transpose(PB[:, e * 128:(e + 1) * 128], oab[:, e * 128:(e + 1) * 128], ident_b[:])
                            return ins
                        S.op("pe", fn, reads=[dtb, cb], writes=[PBb])
                        S.op("act", lambda a: a.activation(out=oaTs[ob_i][:, :, qs * 128:(qs + 1) * 128], in_=PB[:, 0:256].rearrange("p (e f) -> p e f", f=128), func=AF.Copy),
                             reads=[PBb], pwrites=[oaTsb[ob_i]])
                    S.op("sp", lambda g: g.dma_start(out=OAT.ap()[h * 256:(h + 1) * 256, qt * TT:(qt + 1) * TT].rearrange("(e p) t -> p e t", p=128), in_=oaTs[ob_i][:]),
                         reads=[oaTsb[ob_i]], writes=[oatb[h][qt]], dkey="oaTs%d" % ob_i)

            ck(3)
            new_phase()
            stmp, stmpb, aT, aTb, small, smallb, t1, t1b = init_misc()
            iota_i = sb("iota_d", [128, 128], I32)
            DILS = (1, 4, 16)
            HMAX = 64 * 16
            dqT = [sb("dqT%d" % i, [128, T], BF16) for i in range(2)]
            dkl = [sb("dkl%d" % i, [128, T + 2 * HMAX], BF16) for i in range(2)]
            dvl = sb("dvl", [128, T + 2 * HMAX], BF16)
            dva = [sb("dva%d" % i, [128, 32, 144], BF16) for i in range(2)]
            dqTb = [Buf("dqT%d" % i) for i in range(2)]
            dklb = [Buf("dkl%d" % i) for i in range(2)]
            dvlb = Buf("dvl")
            dvab = [Buf("dva%d" % i) for i in range(2)]
            dab = sb("dab", [128, 256], F32)
            mab = sb("mab", [128, 256], F32)
            bab = sb("bab", [128, 256], F32)
            babb = Buf("bab")
            vb = sb("vb", [128, 2], F32)
            S.op("pool", lambda g: g.iota(iota_i[:, 0:128], pattern=[[-1, 128]], base=-64, channel_multiplier=1), writes=[babb])
            S.op("pool", lambda g: g.tensor_copy(out=dab[:, 0:128], in_=iota_i[:, 0:128]), reads=[babb], pwrites=[babb])
            S.op("dve", lambda v: v.tensor_scalar(dab[:, 128:256], dab[:, 0:128], 128.0, None, ALU.add), reads=[babb], pwrites=[babb])
            S.op("act", lambda a: a.activation(out=dab[:], in_=dab[:], func=AF.Abs), reads=[babb], writes=[babb])
            S.op("pool", lambda g: g.memset(mab[:], 0.0), pwrites=[babb])
            S.op("pool", lambda g: g.affine_select(out=mab[:, 0:128], in_=mab[:, 0:128], pattern=[[-1, 128]], compare_op=ALU.is_ge, fill=-1.0e6, base=0, channel_multiplier=1),
                 reads=[babb], writes=[babb])
            S.op("pool", lambda g: g.affine_select(out=mab[:, 128:256], in_=mab[:, 128:256], pattern=[[1, 128]], compare_op=ALU.is_ge, fill=-1.0e6, base=0, channel_multiplier=-1),
                 reads=[babb], writes=[babb])
            S.op("pool", lambda g: g.memset(vb[:], 0.0), reads=[babb], writes=[babb])
            S.op("dve", lambda v: v.tensor_copy(out=vb[0:64, 0:1], in_=meta[0:64, 1:2]), reads=[babb, cb], writes=[babb])
            S.op("dve", lambda v: v.tensor_copy(out=vb[64:128, 1:2], in_=meta[64:128, 2:3]), reads=[babb, cb], writes=[babb])
            for i in range(2):
                S.op("pool", lambda g: g.memset(dva[i][:, :, 128:130], 1.0), writes=[dvab[i]])
            ozs = [sb("ozs%d" % i, [128, 129], F32) for i in range(4)]
            ozsb = [Buf("ozs%d" % i) for i in range(4)]
            ozb = Buf("OZ")
            uk = 0
            gh = 0
            for g_i, dil in enumerate(DILS):
                H = 64 * dil
                Lq = T // dil
                nblk = Lq // 128
                nch = nblk + 1
                for h in range(8):
                    hb = gh % 2
                    gh += 1
                    slope = 2.0 ** (-8.0 * (h * 3 + g_i + 1) / 24)
                    ccq = C_LQ + g_i * 8 + h
                    cck = C_LK + g_i * 8 + h
                    ccv = C_LV + g_i * 8 + h
                    S.op("dve", lambda v: v.scalar_tensor_tensor(bab[:], dab[:], -slope * dil / scale, mab[:], ALU.mult, ALU.add), reads=[babb], writes=[babb])
                    S.op("sp", lambda q: q.dma_start(out=dqT[hb][:], in_=QKV.ap()[ccq * 128:(ccq + 1) * 128, 0:T]), reads=qkvb[ccq], writes=[dqTb[hb]], dkey="dqT%d" % hb)
                    for (dst, bufl, cc) in ((dkl[hb], dklb[hb], cck), (dvl, dvlb, ccv)):
                        S.op("sp", lambda q: q.dma_start(out=dst[:, 0:H], in_=QKV.ap()[cc * 128:(cc + 1) * 128, 2 * T - H:2 * T]), reads=qkvb[cc], writes=[bufl], dkey="dl_" + bufl.name)
                        S.op("sp", lambda q: q.dma_start(out=dst[:, H:H + T], in_=QKV.ap()[cc * 128:(cc + 1) * 128, 0:T]), reads=qkvb[cc], pwrites=[bufl], dkey="dl_" + bufl.name)
                        S.op("sp", lambda q: q.dma_start(out=dst[:, H + T:H + T + H], in_=QKV.ap()[cc * 128:(cc + 1) * 128, T:T + H]), reads=qkvb[cc], pwrites=[bufl], dkey="dl_" + bufl.name)

                    def kcol(r, c):
                        s = r + dil * (128 * c)
                        return slice(s, s + dil * 127 + 1, dil)
                    ids = [(r, c) for r in range(dil) for c in range(nch)]
                    for a0 in range(0, len(ids), 8):
                        grp = ids[a0:a0 + 8]

                        def fn(pe):
                            ins = None
                            for k2, (r, c) in enumerate(grp):
                                ins = pe.transpose(PB[:, k2 * 128:(k2 + 1) * 128], dvl[:, kcol(r, c)], ident_b[:])
                            return ins
                        S.op("pe", fn, reads=[dvlb, cb], writes=[PBb])
                        n8 = len(grp)
                        S.op("act", lambda a: a.activation(out=dva[hb][:, a0:a0 + n8, 0:128], in_=PB[:, 0:n8 * 128].rearrange("p (k f) -> p k f", f=128), func=AF.Copy),
                             reads=[PBb], pwrites=[dvab[hb]])
                    for r in range(dil):
                        for jb in range(nblk):
                            pi = uk % 2
                            oi = uk % 4
                            uk += 1
                            qs0 = r + dil * jb * 128
                            qsl = slice(qs0, qs0 + dil * 127 + 1, dil)

                            def fn(pe):
                                pe.matmul(PS[pi][:, 0:128], lhsT=dkl[hb][:, kcol(r, jb)], rhs=dqT[hb][:, qsl], start=True, stop=True)
                                return pe.matmul(PS[pi][:, 128:256], lhsT=dkl[hb][:, kcol(r, jb + 1)], rhs=dqT[hb][:, qsl], start=True, stop=True)
                            S.op("pe", fn, reads=[dklb[hb], dqTb[hb]], writes=[PSb[pi]])
                            S.op("dve", lambda v: v.tensor_tensor(stmp[pi][:, 0:256], PS[pi][:, 0:256], bab[:], ALU.add), reads=[PSb[pi], babb], writes=[stmpb[pi]])
                            biasA = vb[:, 0:1] if jb == 0 else 0.0
                            biasB = vb[:, 1:2] if jb == nblk - 1 else 0.0
                            S.op("act", lambda a: a.activation(out=aT[pi][:, 0:128], in_=stmp[pi][:, 0:128], func=AF.Exp, scale=scale, bias=biasA), reads=[stmpb[pi], babb], writes=[aTb[pi]])
                            S.op("act", lambda a: a.activation(out=aT[pi][:, 128:256], in_=stmp[pi][:, 128:256], func=AF.Exp, scale=scale, bias=biasB), reads=[stmpb[pi], babb], pwrites=[aTb[pi]])

                            def fn2(pe):
                                pe.matmul(PS[2 + pi][:, 0:129], lhsT=aT[pi][:, 0:128], rhs=dva[hb][:, r * nch + jb, 0:129], start=True, stop=False)
                                return pe.matmul(PS[2 + pi][:, 0:129], lhsT=aT[pi][:, 128:256], rhs=dva[hb][:, r * nch + jb + 1, 0:129], start=False, stop=True)
                            S.op("pe", fn2, reads=[aTb[pi], dvab[hb]], writes=[PSb[2 + pi]])
                            S.op("act", lambda a: a.activation(out=ozs[oi][:], in_=PS[2 + pi][:, 0:129], func=AF.Copy), reads=[PSb[2 + pi]], writes=[ozsb[oi]])
                            colo = (g_i * 8 + h) * 129
                            rsl = slice(qs0, qs0 + dil * 127 + 1, dil)
                            S.op("sp", lambda g: g.dma_start(out=OZ.ap()[rsl, colo:colo + 129], in_=ozs[oi][:]), reads=[ozsb[oi]], pwrites=[ozb], dkey="ozs%d" % oi)
            ozin = [sb("ozin%d" % i, [128, 24, 129], F32) for i in range(2)]
            ozinb = [Buf("ozin%d" % i) for i in range(2)]
            osum = sb("osum", [128, 8, 129], F32)
            osumb = Buf("osum")
            obb = sb("obb", [128, 8, 128], BF16)
            obTs = [sb("obTs%d" % i, [128, 8, 128], BF16) for i in range(2)]
            obTsb = [Buf("obTs%d" % i) for i in range(2)]
            obtb = [Buf("obt_%d" % t) for t in range(T // 128)]
            for tb in range(T // 128):
                i = tb % 2
                S.op("sp", lambda q: q.dma_start(out=ozin[i][:], in_=OZ.ap()[tb * 128:(tb + 1) * 128, :].rearrange("p (a f) -> p a f", f=129)), reads=[ozb], writes=[ozinb[i]], dkey="ozin%d" % i)
                S.op("dve", lambda v: v.tensor_tensor(osum[:], ozin[i][:, 0:8, :], ozin[i][:, 8:16, :], ALU.add), reads=[ozinb[i]], writes=[osumb])
                S.op("dve", lambda v: v.tensor_tensor(osum[:], osum[:], ozin[i][:, 16:24, :], ALU.add), reads=[ozinb[i]], writes=[osumb])
                S.op("dve", lambda v: v.reciprocal(small[:, 0:8], osum[:, :, 128]), reads=[osumb], writes=[smallb])
                for h in range(8):
                    S.op("dve", lambda v: v.tensor_scalar(obb[:, h, :], osum[:, h, 0:128], small[:, h:h + 1], None, ALU.mult), reads=[osumb, smallb], pwrites=[osumb])

                def fn(pe):
                    ins = None
                    for h in range(8):
                        ins = pe.transpose(PB[:, h * 128:(h + 1) * 128], obb[:, h, :], ident_b[:])
                    return ins
                S.op("pe", fn, reads=[osumb, cb], writes=[PBb])
                S.op("act", lambda a: a.activation(out=obTs[i][:], in_=PB[:].rearrange("p (h f) -> p h f", f=128), func=AF.Copy), reads=[PBb], writes=[obTsb[i]])
                S.op("sp", lambda g: g.dma_start(out=OBT.ap()[:, tb * 128:(tb + 1) * 128].rearrange("(h p) t -> p h t", p=128), in_=obTs[i][:]), reads=[obTsb[i]], writes=[obtb[tb]], dkey="obTs%d" % i)

            ck(4)
            new_phase()
            stmp, stmpb, aT, aTb, small, smallb, t1, t1b = init_misc()
            init_arena()
            init_ln()
            init_actin()
            oaT_t = sb("oaT_t", [128, 16, TT], BF16)
            obT_t = sb("obT_t", [128, 8, TT], BF16)
            oaT_tb = Buf("oaT_t")
            obT_tb = Buf("obT_t")
            yT = M["actT_in"]
            yTb = M["actT_inb"]
            sga = [sb("sga%d" % i, [128, TT], BF16) for i in range(2)]
            sgbb_t = [sb("sgbt%d" % i, [128, TT], BF16) for i in range(2)]
            sgab = [Buf("sga%d" % i) for i in range(2)]
            sgbb = [Buf("sgbt%d" % i) for i in range(2)]
            alloat = [b for h in range(8) for b in oatb[h]]
            for tt in range(NTT):
                S.op("sp", lambda q: q.dma_start(out=oaT_t[:], in_=OAT.ap()[:, tt * TT:(tt + 1) * TT].rearrange("(k p) t -> p k t", p=128)), reads=alloat, writes=[oaT_tb], dkey="oaT_t")
                S.op("sp", lambda q: q.dma_start(out=obT_t[:], in_=OBT.ap()[:, tt * TT:(tt + 1) * TT].rearrange("(k p) t -> p k t", p=128)), reads=obtb, writes=[obT_tb], dkey="obT_t")
                for dc in range(KD):
                    i = dc % 2
                    va, ba = load_wtile("wa", dc)
                    vbw, bbw = load_wtile("wb", dc)
                    S.op("pe", mm_group(PS[i][:], "wa", va, 0, W["wa"].KC, lambda kc: oaT_t[:, kc, :]), reads=ba + [oaT_tb], writes=[PSb[i]])
                    S.op("pe", mm_group(PS[2 + i][:], "wb", vbw, 0, W["wb"].KC, lambda kc: obT_t[:, kc, :]), reads=bbw + [obT_tb], writes=[PSb[2 + i]])
                    S.op("sp", lambda q: q.dma_start(out=sga[i][:], in_=QKV.ap()[(C_GA + dc) * 128:(C_GA + dc + 1) * 128, tt * TT:(tt + 1) * TT]), reads=[qkvb[C_GA + dc][tt]], writes=[sgab[i]], dkey="sga%d" % i)
                    S.op("sp", lambda q: q.dma_start(out=sgbb_t[i][:], in_=QKV.ap()[(C_GB + dc) * 128:(C_GB + dc + 1) * 128, tt * TT:(tt + 1) * TT]), reads=[qkvb[C_GB + dc][tt]], writes=[sgbb[i]], dkey="sgbt%d" % i)
                    S.op("dve", lambda v: v.tensor_tensor(t1[i][:], PS[i][:], sga[i][:], ALU.mult), reads=[PSb[i], sgab[i]], writes=[t1b[i]])
                    S.op("dve", lambda v: v.tensor_tensor(stmp[i][:], PS[2 + i][:], sgbb_t[i][:], ALU.mult), reads=[PSb[2 + i], sgbb[i]], writes=[stmpb[i]])
                    if dc == 0:
                        S.op("dve", lambda v: v.tensor_tensor(yT[:, dc, :], t1[i][:], stmp[i][:], ALU.add), reads=[t1b[i], stmpb[i]], writes=[yTb])
                    else:
                        S.op("dve", lambda v: v.tensor_tensor(yT[:, dc, :], t1[i][:], stmp[i][:], ALU.add), reads=[t1b[i], stmpb[i]], pwrites=[yTb])
                for dc in range(KD):
                    vm, bm = load_wtile("wmix", dc)
                    S.op("pe", mm_group(PS[4][:], "wmix", vm, 0, W["wmix"].KC, lambda kc: yT[:, kc, :]), reads=bm + [yTb], writes=[PSb[4]])
                    epilogue_resid(4, lambda: PS[4][:], [PSb[4]], HF[0], HFb[0], tt, dc)
                ln_finish(tt, 1, HF[1], HFb[1], HB[1], HBb[1])

            ck(5)
            new_phase()
            ffn_phase("f2i", "f2o", HB[1], HBb[1], HF[1], HFb[1], 2, HF[2], HFb[2], HB[2], HBb[2], NTT)

            ck(6)
            new_phase()
            stmp, stmpb, aT, aTb, small, smallb, t1, t1b = init_misc()
            init_arena()
            init_ln()
            init_actin()
            pT_t = sb("pT_t", [128, 2, TT], BF16)
            pT_tb = Buf("pT_t")
            for tt in range(NTT):
                load_act_tile(HB[2], flat(HBb[2], tt), tt * TT)
                S.op("pool", lambda q: q.dma_start(out=pT_t[:], in_=pT_in.ap()[:, tt * TT:(tt + 1) * TT].rearrange("(k p) t -> p k t", p=128)), writes=[pT_tb], dkey="pT_t")
                for dc in range(KD):
                    i = dc % 2
                    vg, bg = load_wtile("wpg", dc)
                    vp, bp = load_wtile("wpp", dc)
                    S.op("pe", mm_group(PS[i][:], "wpg", vg, 0, W["wpg"].KC, lambda kc: M["actT_in"][:, kc, :]), reads=bg + [M["actT_inb"]], writes=[PSb[i]])
                    S.op("pe", mm_group(PS[2 + i][:], "wpp", vp, 0, W["wpp"].KC, lambda kc: pT_t[:, kc, :]), reads=bp + [pT_tb], writes=[PSb[2 + i]])
                    S.op("act", lambda a: a.activation(out=t1[i][:], in_=PS[i][:], func=AF.Sigmoid), reads=[PSb[i]], writes=[t1b[i]])
                    S.op("dve", lambda v: v.tensor_tensor(stmp[i][:], t1[i][:], PS[2 + i][:], ALU.mult), reads=[t1b[i], PSb[2 + i]], writes=[stmpb[i]])
                    epilogue_resid(None, lambda: stmp[i][:], [stmpb[i]], HF[2], HFb[2], tt, dc)
                ln_finish(tt, 3, out_T, outb, None, None)
        except StopBuild:
            pass
        if cfg.get("dbg"):
            srcd = dict(h1=HF[0], h2=HF[1], h3=HF[2])[cfg["dbg"]]
            S.barrier()
            S.op("sp", lambda q: q.dma_start(out=out_T.ap(), in_=srcd.ap()[:, 0:T]), dkey="dbg")
        S.barrier()
        pstack[0].close()
    return nc


_CACHE = {}


def _host_inputs(cfg, inp):
    DM, T = cfg["DM"], cfg["T"]
    KD = DM // 128
    f32 = np.float32

    def lay(v):
        return np.ascontiguousarray(np.asarray(v, f32).reshape(KD, 128).T)
    lnp = np.concatenate([lay(inp[k][0]) for k in ("ln1_g", "ln1_b", "ln2_g", "ln2_b", "ln3_g", "ln3_b", "ln4_g", "ln4_b")], axis=1)
    lamv = np.concatenate([np.broadcast_to(np.asarray(inp[k][0], f32)[None, :], (128, 128)) for k in ("lam_q1", "lam_k1", "lam_q2", "lam_k2")], axis=1)
    subg = np.ascontiguousarray(np.broadcast_to(np.asarray(inp["subln_g"][0], f32)[None, :], (128, 256)))
    wsrc = dict(f1i="ffn1_w_in", f1o="ffn1_w_out", win="w_in", wa="w_branch_diff", wb="w_branch_dil", wmix="w_mix_out",
                f2i="ffn2_w_in", f2o="ffn2_w_out", wpg="w_ple_gate", wpp="w_ple_proj")
    wt = {}
    for n, src in wsrc.items():
        w = np.asarray(inp[src], f32)[0]
        K_, N_ = w.shape
        wt[n] = np.ascontiguousarray(w.reshape(K_ // 128, 128, N_ // 128, 128).transpose(2, 1, 0, 3)).reshape(N_ // 128, 128, K_)
    maps = []
    x = np.asarray(inp["x"], f32)
    p = np.asarray(inp["p"], f32)[0]
    for c in range(NCORES):
        b, half = c // 2, c % 2
        m = {}
        xl = np.concatenate([x[b, half * T:(half + 1) * T, :], x[b, (1 - half) * T:(2 - half) * T, :]], axis=0)
        m["xT"] = np.ascontiguousarray(xl.T)
        m["pT"] = np.ascontiguousarray(p[b, half * T:(half + 1) * T, :].T)
        meta = np.zeros((128, 16), f32)
        meta[:, 0] = (2 * half - 1) * T
        meta[:, 1] = 0.0 if half == 1 else -30000.0
        meta[:, 2] = 0.0 if half == 0 else -30000.0
        meta[:, 3 + (c ^ 1)] = 1.0
        m["meta"] = meta
        m["lnp"] = np.ascontiguousarray(lnp)
        m["lamv"] = np.ascontiguousarray(lamv)
        m["subg"] = subg
        for n in wsrc:
            m["w_" + n] = wt[n]
        maps.append(m)
    return maps


def run(cfg, inp, trace=False):
    key = (cfg["DM"], cfg["DFF"], cfg["T"])
    if key not in _CACHE:
        _CACHE[key] = build(cfg)
    nc = _CACHE[key]
    maps = _host_inputs(cfg, inp)
    res = run_bass_kernel_spmd(nc, maps, core_ids=list(range(NCORES)))
    T, DM = cfg["T"], cfg["DM"]
    out = np.empty((4, 2 * T, DM), np.float32)
    for c in range(NCORES):
        b, half = c // 2, c % 2
        out[b, half * T:(half + 1) * T, :] = res.results[c]["outT"].T
    return out


def kernel(**inputs):
    cfg = make_cfg()
    return run(cfg, inputs)
```

```python
import math
from contextlib import ExitStack
import numpy as np
import concourse.bass as bass
import concourse.mybir as mybir
from concourse.bass_utils import run_bass_kernel_spmd

F32 = mybir.dt.float32
BF16 = mybir.dt.bfloat16
I32 = mybir.dt.int32
AF = mybir.ActivationFunctionType
ALU = mybir.AluOpType
AX = mybir.AxisListType

NCORES = 8
TT = 512


def make_cfg(DM=4096, DFF=11008, T=2048):
    c = dict(DM=DM, DFF=DFF, T=T, PLE=256, NH=8, DH=128)
    c["QKW"] = 2048
    c["DVW"] = 2048
    c["DILW"] = 3072
    c["INW"] = 2 * 2048 + 2048 + 3 * 3072 + 2 * DM
    return c


class StopBuild(Exception):
    pass


class Buf:
    __slots__ = ("name", "w", "r")

    def __init__(self, name):
        self.name = name
        self.w = {}
        self.r = {}


class Sched:
    def __init__(self, nc, es):
        self.nc = nc
        self.es = es
        self.E = dict(pe=nc.tensor, act=nc.scalar, dve=nc.vector, pool=nc.gpsimd, sp=nc.sync)
        self.S = {}
        self.val = {}
        self.seen = {e: {} for e in self.E}
        for e in ("pe", "act", "dve", "pool"):
            self._mk(e)

    def _mk(self, key):
        self.S[key] = self.es.enter_context(self.nc.semaphore("s_" + key))
        self.val[key] = 0

    def op(self, e, fn, reads=(), writes=(), pwrites=(), dkey=None, inc=16):
        deps = {}

        def add(d):
            for k, v in d.items():
                if deps.get(k, 0) < v:
                    deps[k] = v

        for b in reads:
            add(b.w)
        for b in writes:
            add(b.w)
            add(b.r)
        for b in pwrites:
            add(b.r)
        eng = self.E[e]
        seen = self.seen[e]
        for k, v in deps.items():
            if k == e and e == "pe" and dkey is None:
                continue
            if seen.get(k, 0) >= v:
                continue
            eng.wait_ge(self.S[k], v)
            seen[k] = v
        ins = fn(eng)
        if dkey is not None:
            if dkey not in self.S:
                self._mk(dkey)
            self.val[dkey] += inc
            if inc == 1:
                ins.then_inc(self.S[dkey])
            else:
                ins.then_inc(self.S[dkey], inc)
            ev = (dkey, self.val[dkey])
        else:
            self.val[e] += 1
            ins.then_inc(self.S[e], 1)
            ev = (e, self.val[e])
        for b in reads:
            if b.r.get(ev[0], 0) < ev[1]:
                b.r[ev[0]] = ev[1]
        for b in writes:
            b.w = {ev[0]: ev[1]}
            b.r = {}
        for b in pwrites:
            if b.w.get(ev[0], 0) < ev[1]:
                b.w[ev[0]] = ev[1]
        return ev

    def barrier(self):
        for e, eng in self.E.items():
            for k, v in self.val.items():
                if v > 0 and self.seen[e].get(k, 0) < v:
                    eng.wait_ge(self.S[k], v)
                    self.seen[e][k] = v


class WSpec:
    def __init__(self, name, K, ncc):
        self.name = name
        self.K = K
        self.KC = K // 128
        self.ncc = ncc
        self.ksz = [128] * self.KC


def _groups(nchunks):
    out = []
    i = 0
    while i < len(nchunks):
        c0, w = nchunks[i]
        n = 1
        if w == 128:
            while i + n < len(nchunks) and nchunks[i + n] == (c0 + 128 * n, 128) and n < 32:
                n += 1
        out.append((i, c0, n, w))
        i += n
    return out


def build(cfg):
    DM, DFF, T = cfg["DM"], cfg["DFF"], cfg["T"]
    KD = DM // 128
    NTT = T // TT
    INW = cfg["INW"]
    alpha = 2.0 ** 0.25
    lambda_init = 0.8 - 0.6 * math.exp(-0.3 * 0)
    scale = 128.0 ** -0.5
    NF = DFF // 128
    fchunks = [(128 * i, 128) for i in range(NF)]
    NLT = 2 * NTT

    nc = bass.Bass("TRN2", target_bir_lowering=False)

    def din(name, shape, dt=F32):
        return nc.dram_tensor(name, list(shape), dt, kind="ExternalInput")

    def dscr(name, shape, dt=F32):
        return nc.dram_tensor(name, list(shape), dt)

    xT_in = din("xT", [DM, 2 * T])
    pT_in = din("pT", [256, T])
    meta_in = din("meta", [128, 16])
    lnp_in = din("lnp", [128, 8 * KD])
    lamv_in = din("lamv", [128, 4 * 128])
    subg_in = din("subg", [128, 256])
    out_T = nc.dram_tensor("outT", [DM, T], F32, kind="ExternalOutput")

    W = {}
    W["f1i"] = WSpec("f1i", DM, 2 * NF)
    W["f1o"] = WSpec("f1o", DFF, KD)
    W["win"] = WSpec("win", DM, INW // 128)
    W["wa"] = WSpec("wa", 2048, KD)
    W["wb"] = WSpec("wb", 1024, KD)
    W["wmix"] = WSpec("wmix", DM, KD)
    W["f2i"] = WSpec("f2i", DM, 2 * NF)
    W["f2o"] = WSpec("f2o", DFF, KD)
    W["wpg"] = WSpec("wpg", DM, KD)
    W["wpp"] = WSpec("wpp", 256, KD)
    worder = ["f1i", "f1o", "win", "wa", "wb", "wmix", "f2i", "f2o", "wpg", "wpp"]
    w_in_d = {}
    for n in worder:
        w_in_d[n] = din("w_" + n, [W[n].ncc, 128, W[n].KC * 128])

    RT = dscr("RT", [DM, 2 * T])
    HF = [dscr("HF%d" % i, [DM, 2 * T if i == 0 else T]) for i in range(3)]
    HB = [dscr("HB%d" % i, [DM, 2 * T if i == 0 else T], BF16) for i in range(3)]
    QKV = dscr("QKV", [INW, 2 * T], BF16)
    OAT = dscr("OAT", [2048, T], BF16)
    OBT = dscr("OBT", [1024, T], BF16)
    OZ = dscr("OZ", [T, 24 * 129])

    es = ExitStack()
    with es:
        S = Sched(nc, es)

        pstack = [ExitStack()]
        uniq = [0]

        def sbp(name, shape, dt):
            return es.enter_context(nc.sbuf_tensor("sb_" + name, list(shape), dt))

        def sb(name, shape, dt):
            uniq[0] += 1
            return pstack[0].enter_context(nc.sbuf_tensor("sb_%s_%d" % (name, uniq[0]), list(shape), dt))

        def new_phase():
            S.barrier()
            pstack[0].close()
            pstack[0] = ExitStack()

        PS = [es.enter_context(nc.psum_tensor("P%d" % i, [128, 512], F32)) for i in range(7)]
        PB = es.enter_context(nc.psum_tensor("PB", [128, 1024], BF16))
        PSb = [Buf("P%d" % i) for i in range(7)]
        PBb = Buf("PB")

        ident_f = sbp("ident_f", [128, 128], F32)
        ident_b = sbp("ident_b", [128, 128], BF16)
        ones_f = sbp("ones_f", [128, 128], F32)
        meta = sbp("meta", [128, 16], F32)
        lnp = sbp("lnp", [128, 8 * KD], F32)
        eps_t = sbp("eps_t", [128, 1], F32)
        cb = Buf("consts")
        S.op("pool", lambda g: g.memset(ident_f[:], 0.0), writes=[cb])
        S.op("pool", lambda g: g.affine_select(out=ident_f[:], in_=ident_f[:], pattern=[[-1, 128]],
                                               compare_op=ALU.not_equal, fill=1.0, base=0, channel_multiplier=1),
             writes=[cb])
        S.op("pool", lambda g: g.tensor_copy(out=ident_b[:], in_=ident_f[:]), reads=[cb], pwrites=[cb])
        S.op("pool", lambda g: g.memset(ones_f[:], 1.0), pwrites=[cb])
        S.op("pool", lambda g: g.memset(eps_t[:], 1e-5), pwrites=[cb])
        S.op("sp", lambda q: q.dma_start(out=meta[:], in_=meta_in[:, :]), pwrites=[cb], dkey="c_meta")
        S.op("sp", lambda q: q.dma_start(out=lnp[:], in_=lnp_in[:, :]), pwrites=[cb], dkey="c_lnp")

        NSLOT = 5
        M = {}
        slot_i = [0]

        def init_arena():
            M["arena"] = sb("arena", [128, NSLOT * 4096], BF16)
            M["slotb"] = [Buf("slot%d" % i) for i in range(NSLOT)]
            slot_i[0] = 0

        def load_wtile(n, cc, kc0=0, kc1=None):
            w = W[n]
            if kc1 is None:
                kc1 = w.KC
            nk = kc1 - kc0
            nsl = (nk * 128 + 4095) // 4096
            s0 = slot_i[0] % NSLOT
            if s0 + nsl > NSLOT:
                s0 = 0
            slot_i[0] = s0 + nsl
            arena = M["arena"]
            bufs = M["slotb"][s0:s0 + nsl]
            base = s0 * 4096
            dstv = arena[:, base:base + nk * 128]
            S.op("pool", lambda q: q.dma_start(out=dstv, in_=w_in_d[n].ap()[cc, :, kc0 * 128:kc1 * 128]), writes=bufs,
                 dkey="wl%d" % s0)

            def view(kc, ncols=128):
                k = kc - kc0
                return arena[:w.ksz[kc], base + k * 128: base + k * 128 + ncols]
            return view, bufs

        def mm_group(ps_ap, n, view, kc0, kc1, act, first=True, last=True):
            w = W[n]

            def fn(pe):
                ins = None
                for kc in range(kc0, kc1):
                    ins = pe.matmul(ps_ap, lhsT=view(kc), rhs=act(kc), start=(first and kc == kc0),
                                    stop=(last and kc == kc1 - 1))
                return ins
            return fn

        NR = 3
        L = {}
        cnt = dict(r=0, sq=0, z=0, res=0)
        P_SUM, P_SQ = 5, 6
        rtb = [[Buf("RT_%d_%d" % (t, d)) for d in range(KD)] for t in range(NLT)]

        def init_ln():
            L["rbuf"] = [sb("rbuf%d" % i, [128, TT], F32) for i in range(NR)]
            L["rbb"] = [Buf("rbuf%d" % i) for i in range(NR)]
            L["sqbuf"] = [sb("sqbuf%d" % i, [128, TT], F32) for i in range(2)]
            L["sqb"] = [Buf("sqbuf%d" % i) for i in range(2)]
            L["mean_t"] = sb("mean_t", [128, TT], F32)
            L["rstd_t"] = sb("rstd_t", [128, TT], F32)
            L["lntmp"] = sb("lntmp", [128, TT], F32)
            L["mrb"] = Buf("meanrstd")
            L["zin"] = [sb("zin%d" % i, [128, TT], F32) for i in range(2)]
            L["zinb"] = [Buf("zin%d" % i) for i in range(2)]
            L["hof"] = [sb("hof%d" % i, [128, TT], F32) for i in range(2)]
            L["hofb"] = [Buf("hof%d" % i) for i in range(2)]
            L["hob"] = [sb("hob%d" % i, [128, TT], BF16) for i in range(2)]
            L["hobb"] = [Buf("hob%d" % i) for i in range(2)]
            L["resid"] = [sb("resid%d" % i, [128, TT], F32) for i in range(2)]
            L["residb"] = [Buf("resid%d" % i) for i in range(2)]

        def init_actin():
            M["actT_in"] = sb("actT_in", [128, KD, TT], BF16)
            M["actT_inb"] = Buf("actT_in")

        def init_misc():
            M["stmp"] = [sb("stmp%d" % i, [128, TT], F32) for i in range(2)]
            M["stmpb"] = [Buf("stmp%d" % i) for i in range(2)]
            M["aT"] = [sb("aT%d" % i, [128, TT], BF16) for i in range(2)]
            M["aTb"] = [Buf("aT%d" % i) for i in range(2)]
            M["small"] = sb("small", [128, 16], F32)
            M["smallb"] = Buf("small")
            M["t1"] = [sb("t1_%d" % i, [128, TT], F32) for i in range(2)]
            M["t1b"] = [Buf("t1_%d" % i) for i in range(2)]
            return M["stmp"], M["stmpb"], M["aT"], M["aTb"], M["small"], M["smallb"], M["t1"], M["t1b"]

        def load_resid(src_dram, srcbufs, dc, tt):
            i = cnt["res"] % 2
            cnt["res"] += 1
            S.op("sp", lambda q: q.dma_start(out=L["resid"][i][:], in_=src_dram.ap()[dc * 128:(dc + 1) * 128, tt * TT:(tt + 1) * TT]),
                 reads=srcbufs, writes=[L["residb"][i]], dkey="resid%d" % i)
            return i

        def ln_piece(tt, dc, i):
            S.op("sp", lambda g: g.dma_start(out=RT.ap()[dc * 128:(dc + 1) * 128, tt * TT:(tt + 1) * TT], in_=L["rbuf"][i][:]),
                 reads=[L["rbb"][i]], writes=[rtb[tt][dc]], dkey="rst%d" % i)
            j = cnt["sq"] % 2
            cnt["sq"] += 1
            S.op("act", lambda a: a.activation(out=L["sqbuf"][j][:], in_=L["rbuf"][i][:], func=AF.Square), reads=[L["rbb"][i]], writes=[L["sqb"][j]])

            def fn(pe):
                pe.matmul(PS[P_SUM][:], lhsT=ones_f[:], rhs=L["rbuf"][i][:], start=(dc == 0), stop=(dc == KD - 1))
                return pe.matmul(PS[P_SQ][:], lhsT=ones_f[:], rhs=L["sqbuf"][j][:], start=(dc == 0), stop=(dc == KD - 1))
            if dc == 0:
                S.op("pe", fn, reads=[L["rbb"][i], L["sqb"][j], cb], writes=[PSb[P_SUM], PSb[P_SQ]])
            else:
                S.op("pe", fn, reads=[L["rbb"][i], L["sqb"][j], cb], pwrites=[PSb[P_SUM], PSb[P_SQ]])

        def ln_finish_gen(tt, gi, dstF, dstFb, dstB, dstBb):
            S.op("dve", lambda v: v.tensor_scalar(L["mean_t"][:], PS[P_SUM][:], 1.0 / DM, None, ALU.mult), reads=[PSb[P_SUM]], writes=[L["mrb"]])
            S.op("dve", lambda v: v.tensor_tensor(L["lntmp"][:], L["mean_t"][:], L["mean_t"][:], ALU.mult), reads=[L["mrb"]], pwrites=[L["mrb"]])
            S.op("dve", lambda v: v.scalar_tensor_tensor(L["rstd_t"][:], PS[P_SQ][:], 1.0 / DM, L["lntmp"][:], ALU.mult, ALU.subtract),
                 reads=[PSb[P_SQ], L["mrb"]], pwrites=[L["mrb"]])
            S.op("act", lambda a: a.activation(out=L["lntmp"][:], in_=L["rstd_t"][:], func=AF.Sqrt, bias=eps_t[:], scale=1.0), reads=[L["mrb"], cb], pwrites=[L["mrb"]])
            S.op("dve", lambda v: v.reciprocal(L["rstd_t"][:], L["lntmp"][:]), reads=[L["mrb"]], pwrites=[L["mrb"]])

            def issue_load(dc):
                i = cnt["z"] % 2
                cnt["z"] += 1
                sl = (slice(dc * 128, (dc + 1) * 128), slice(tt * TT, (tt + 1) * TT))
                S.op("sp", lambda q: q.dma_start(out=L["zin"][i][:], in_=RT.ap()[sl]), reads=[rtb[tt][dc]], writes=[L["zinb"][i]], dkey="zin%d" % i)
                return i
            nxt = issue_load(0)
            yield
            for dc in range(KD):
                i = nxt
                if dc + 1 < KD:
                    nxt = issue_load(dc + 1)
                sl = (slice(dc * 128, (dc + 1) * 128), slice(tt * TT, (tt + 1) * TT))
                S.op("dve", lambda v: v.tensor_tensor(L["zin"][i][:], L["zin"][i][:], L["mean_t"][:], ALU.subtract), reads=[L["mrb"]], writes=[L["zinb"][i]])
                S.op("dve", lambda v: v.tensor_tensor(L["zin"][i][:], L["zin"][i][:], L["rstd_t"][:], ALU.mult), reads=[L["mrb"]], writes=[L["zinb"][i]])
                g_ap = lnp[:, (2 * gi) * KD + dc:(2 * gi) * KD + dc + 1]
                b_ap = lnp[:, (2 * gi + 1) * KD + dc:(2 * gi + 1) * KD + dc + 1]
                S.op("act", lambda a: a.activation(out=L["hof"][i][:], in_=L["zin"][i][:], func=AF.Identity, bias=b_ap, scale=g_ap),
                     reads=[L["zinb"][i], cb], writes=[L["hofb"][i]])
                S.op("sp", lambda g: g.dma_start(out=dstF.ap()[sl], in_=L["hof"][i][:]), reads=[L["hofb"][i]], writes=[dstFb[tt][dc]], dkey="hof%d" % i)
                if dstB is not None:
                    S.op("act", lambda a: a.activation(out=L["hob"][i][:], in_=L["hof"][i][:], func=AF.Copy), reads=[L["hofb"][i]], writes=[L["hobb"][i]])
                    S.op("sp", lambda g: g.dma_start(out=dstB.ap()[sl], in_=L["hob"][i][:]), reads=[L["hobb"][i]], writes=[dstBb[tt][dc]], dkey="hob%d" % i)
                yield

        def ln_start(*args):
            g = ln_finish_gen(*args)
            next(g)
            return g

        def ln_step(g):
            if g is not None:
                next(g, None)

        def ln_drain(g):
            if g is not None:
                for _ in g:
                    pass

        def epilogue_resid(ps_i, src_ap_fn, srcbufs, resF, resFb, tt, dc):
            ri = load_resid(resF, [resFb[tt][dc]], dc, tt)
            i = cnt["r"] % NR
            cnt["r"] += 1
            S.op("dve", lambda v: v.scalar_tensor_tensor(L["rbuf"][i][:], L["resid"][ri][:], alpha, src_ap_fn(), ALU.mult, ALU.add),
                 reads=[L["residb"][ri]] + srcbufs, writes=[L["rbb"][i]])
            ln_piece(tt, dc, i)


        def load_act_tile(src_dram, srcbufs, col0, cast=False):
            S.op("pool" if cast else "sp", lambda q: q.dma_start(out=M["actT_in"][:], in_=src_dram.ap()[:, col0:col0 + TT].rearrange("(k p) t -> p k t", p=128)),
                 reads=srcbufs, writes=[M["actT_inb"]], dkey="actin")

        def dbufs(name):
            return [[Buf("%s_%d_%d" % (name, t, d)) for d in range(KD)] for t in range(NLT)]

        def flat(bb, tt):
            return bb[tt]

        xin_b = [[Buf("xin")] * KD for _ in range(NLT)]

        def ffn_phase(wi, wo, srcB, srcBb, resF, resFb, gi, dstF, dstFb, dstB, dstBb, ntiles, cast=False):
            wI, wO = W[wi], W[wo]
            init_arena()
            init_ln()
            init_actin()
            actT = sb("actT", [128, NF, TT], BF16)
            actTb = Buf("actT")
            sgbuf = [sb("sg%d" % i, [128, TT], F32) for i in range(2)]
            sgb = [Buf("sg%d" % i) for i in range(2)]
            it = 0
            pending = None
            step = max(1, NF // (KD + 1))
            load_act_tile(srcB, flat(srcBb, 0), 0, cast)
            for tt in range(ntiles):
                for f in range(NF):
                    fw = fchunks[f][1]
                    vg, bg = load_wtile(wi, f)
                    vu, bu = load_wtile(wi, NF + f)
                    pg, pu = (0, 1) if it % 2 == 0 else (2, 3)
                    it += 1
                    S.op("pe", mm_group(PS[pg][:fw, :], wi, lambda kc, v=vg: v(kc, fw), 0, wI.KC, lambda kc: M["actT_in"][:, kc, :]),
                         reads=bg + [M["actT_inb"]], writes=[PSb[pg]])
                    S.op("pe", mm_group(PS[pu][:fw, :], wi, lambda kc, v=vu: v(kc, fw), 0, wI.KC, lambda kc: M["actT_in"][:, kc, :]),
                         reads=bu + [M["actT_inb"]], writes=[PSb[pu]])
                    j = it % 2
                    S.op("act", lambda a: a.activation(out=sgbuf[j][:fw, :], in_=PS[pg][:fw, :], func=AF.Silu), reads=[PSb[pg]], writes=[sgb[j]])
                    S.op("dve", lambda v: v.scalar_tensor_tensor(actT[:fw, f, :], sgbuf[j][:fw, :], 0.5, PS[pu][:fw, :], ALU.mult, ALU.mult),
                         reads=[sgb[j], PSb[pu]], pwrites=[actTb])
                    if f % step == step - 1:
                        ln_step(pending)
                ln_drain(pending)
                pending = None
                if tt + 1 < ntiles:
                    load_act_tile(srcB, flat(srcBb, tt + 1), (tt + 1) * TT, cast)
                half = wO.KC // 2
                for dc in range(KD):
                    v1, b1 = load_wtile(wo, dc, 0, half)
                    v2, b2 = load_wtile(wo, dc, half, wO.KC)
                    S.op("pe", mm_group(PS[4][:], wo, v1, 0, half, lambda kc: actT[:wO.ksz[kc], kc, :], True, False),
                         reads=b1 + [actTb], writes=[PSb[4]])
                    S.op("pe", mm_group(PS[4][:], wo, v2, half, wO.KC, lambda kc: actT[:wO.ksz[kc], kc, :], False, True),
                         reads=b2 + [actTb], pwrites=[PSb[4]])
                    epilogue_resid(4, lambda: PS[4][:], [PSb[4]], resF, resFb, tt, dc)
                pending = ln_start(tt, gi, dstF, dstFb, dstB, dstBb)
            ln_drain(pending)

        HFb = [dbufs("HF%d" % i) for i in range(3)]
        HBb = [dbufs("HB%d" % i) for i in range(3)]
        outb = dbufs("out")

        def ck(k):
            if cfg.get('stop') == k:
                raise StopBuild()

        try:
            ck(0)
            ffn_phase("f1i", "f1o", xT_in, xin_b, xT_in, xin_b, 0, HF[0], HFb[0], HB[0], HBb[0], NLT, True)

            ck(1)
            new_phase()
            init_arena()
            init_actin()
            QKW, DILW = cfg["QKW"], cfg["DILW"]
            C_DQ, C_DK, C_DV = 0, QKW // 128, 2 * QKW // 128
            C_LQ = C_DV + 2048 // 128
            C_LK = C_LQ + DILW // 128
            C_LV = C_LK + DILW // 128
            C_GA = C_LV + DILW // 128
            C_GB = C_GA + KD
            NCC = C_GB + KD
            kv_ccs = list(range(C_DK, C_LQ)) + list(range(C_LK, C_GA))
            qkvb = [[Buf("qkv_%d_%d" % (cc, t)) for t in range(2 * NTT)] for cc in range(NCC)]
            ost = [sb("ost%d" % i, [128, TT], BF16) for i in range(2)]
            ostb = [Buf("ost%d" % i) for i in range(2)]
            k = 0
            for lt in range(2 * NTT):
                own = lt < NTT
                load_act_tile(HB[0], flat(HBb[0], lt), lt * TT)
                for cc in (range(NCC) if own else kv_ccs):
                    v, b = load_wtile("win", cc)
                    pi = k % 4
                    i = k % 2
                    k += 1
                    S.op("pe", mm_group(PS[pi][:], "win", v, 0, W["win"].KC, lambda kc: M["actT_in"][:, kc, :]), reads=b + [M["actT_inb"]], writes=[PSb[pi]])
                    if cc >= C_GA:
                        S.op("act", lambda a: a.activation(out=ost[i][:], in_=PS[pi][:], func=AF.Sigmoid), reads=[PSb[pi]], writes=[ostb[i]])
                    elif k % 2 == 0:
                        S.op("act", lambda a: a.activation(out=ost[i][:], in_=PS[pi][:], func=AF.Copy), reads=[PSb[pi]], writes=[ostb[i]])
                    else:
                        S.op("dve", lambda vv: vv.tensor_copy(out=ost[i][:], in_=PS[pi][:]), reads=[PSb[pi]], writes=[ostb[i]])
                    S.op("sp", lambda g: g.dma_start(out=QKV.ap()[cc * 128:(cc + 1) * 128, lt * TT:(lt + 1) * TT], in_=ost[i][:]),
                         reads=[ostb[i]], writes=[qkvb[cc][lt]], dkey="ost%d" % i)

            ck(2)
            new_phase()
            stmp, stmpb, aT, aTb, small, smallb, t1, t1b = init_misc()
            NKC = 2 * T // 128
            NQT = T // TT
            CA = T - 128
            TW = 2 * T - 128
            iota_i = sb("iota_i", [128, TW], I32)
            tabA = sb("tabA", [128, TW], F32)
            tabB = sb("tabB", [128, TW], F32)
            tabb = Buf("tabs")
            S.op("pool", lambda g: g.iota(iota_i[:], pattern=[[1, TW]], base=-CA, channel_multiplier=-1), writes=[tabb])
            S.op("pool", lambda g: g.tensor_copy(out=tabA[:], in_=iota_i[:]), reads=[tabb], pwrites=[tabb])
            S.op("act", lambda a: a.activation(out=tabB[:], in_=tabA[:], func=AF.Abs, bias=meta[:, 0:1], scale=1.0), reads=[tabb, cb], pwrites=[tabb])
            S.op("act", lambda a: a.activation(out=tabA[:], in_=tabA[:], func=AF.Abs), reads=[tabb], writes=[tabb])
            lamv = sb("lamv", [128, 512], F32)
            lamj = sb("lamj", [128, 128], F32)
            lams = sb("lams", [128, 4], F32)
            neglam = sb("neglam", [128, 1], F32)
            gs_t = sb("gs_t", [128, 256], F32)
            lamb = Buf("lam")
            S.op("sp", lambda q: q.dma_start(out=lamv[:], in_=lamv_in[:, :]), writes=[lamb], dkey="c_lam")
            S.op("sp", lambda q: q.dma_start(out=gs_t[:], in_=subg_in[:, :]), pwrites=[lamb], dkey="c_subg")
            S.op("dve", lambda v: v.memset(lams[:], 0.0), pwrites=[lamb])
            for j in range(2):
                S.op("dve", lambda v: v.tensor_tensor(lamj[:], lamv[:, (2 * j) * 128:(2 * j + 1) * 128], lamv[:, (2 * j + 1) * 128:(2 * j + 2) * 128], ALU.mult),
                     reads=[lamb], pwrites=[lamb])
                S.op("dve", lambda v: v.reduce_sum(lams[:, j:j + 1], lamj[:], AX.X), reads=[lamb], pwrites=[lamb])
            S.op("act", lambda a: a.activation(out=lams[:, 2:4], in_=lams[:, 0:2], func=AF.Exp), reads=[lamb], pwrites=[lamb])
            S.op("dve", lambda v: v.tensor_tensor(neglam[:], lams[:, 3:4], lams[:, 2:3], ALU.subtract), reads=[lamb], pwrites=[lamb])
            S.op("dve", lambda v: v.tensor_scalar(neglam[:], neglam[:], -lambda_init, None, ALU.add), reads=[lamb], pwrites=[lamb])
            S.op("dve", lambda v: v.tensor_scalar(gs_t[:], gs_t[:], 1.0 - lambda_init, None, ALU.mult), reads=[lamb], pwrites=[lamb])

            qT = [sb("qT%d" % i, [128, 2, T], BF16) for i in range(2)]
            kT = [sb("kT%d" % i, [128, 2, 2 * T], BF16) for i in range(2)]
            vT = sb("vT", [128, 2, 2 * T], BF16)
            vaug = [sb("vaug%d" % i, [128, NKC, 272], BF16) for i in range(2)]
            qTb = [Buf("qT%d" % i) for i in range(2)]
            kTb = [Buf("kT%d" % i) for i in range(2)]
            vTb = Buf("vT")
            vaugb = [Buf("vaug%d" % i) for i in range(2)]
            U = sb("U", [128, 2, 4, 256], F32)
            Ub = Buf("U")
            dtmp = sb("dtmp", [128, 256], F32)
            djunk = sb("djunk", [128, 256], F32)
            oab = sb("oab", [128, 256], BF16)
            dtb = Buf("dtmp")
            oaTs = [sb("oaTs%d" % i, [128, 2, TT], BF16) for i in range(2)]
            oaTsb = [Buf("oaTs%d" % i) for i in range(2)]
            oatb = [[Buf("oat_%d_%d" % (h, t)) for t in range(NQT)] for h in range(8)]
            for i in range(2):
                S.op("pool", lambda g: g.memset(vaug[i][:, :, 256:258], 1.0), writes=[vaugb[i]])

            def rows(cc0, n):
                return [b for cc in range(cc0, cc0 + n) for b in qkvb[cc]]

            for h in range(8):
                hb = h % 2
                slope = 2.0 ** (-8.0 * (h + 1) / 8)
                S.op("sp", lambda q: q.dma_start(out=qT[hb][:], in_=QKV.ap()[(C_DQ + 2 * h) * 128:(C_DQ + 2 * h + 2) * 128, 0:T].rearrange("(c p) t -> p c t", p=128)),
                     reads=rows(C_DQ + 2 * h, 2), writes=[qTb[hb]], dkey="qT%d" % hb)
                S.op("sp", lambda q: q.dma_start(out=kT[hb][:], in_=QKV.ap()[(C_DK + 2 * h) * 128:(C_DK + 2 * h + 2) * 128, :].rearrange("(c p) t -> p c t", p=128)),
                     reads=rows(C_DK + 2 * h, 2), writes=[kTb[hb]], dkey="kT%d" % hb)
                S.op("sp", lambda q: q.dma_start(out=vT[:], in_=QKV.ap()[(C_DV + 2 * h) * 128:(C_DV + 2 * h + 2) * 128, :].rearrange("(c p) t -> p c t", p=128)),
                     reads=rows(C_DV + 2 * h, 2), writes=[vTb], dkey="vT")
                for kc0 in range(0, NKC, 4):
                    def fn(pe):
                        ins = None
                        for kk in range(4):
                            for e in range(2):
                                ins = pe.transpose(PB[:, (kk * 2 + e) * 128:(kk * 2 + e + 1) * 128], vT[:, e, (kc0 + kk) * 128:(kc0 + kk + 1) * 128], ident_b[:])
                        return ins
                    S.op("pe", fn, reads=[vTb, cb], writes=[PBb])
                    S.op("act", lambda a: a.activation(out=vaug[hb][:, kc0:kc0 + 4, 0:256], in_=PB[:].rearrange("p (k f) -> p k f", f=256), func=AF.Copy),
                         reads=[PBb], pwrites=[vaugb[hb]])
                for qt in range(NQT):
                    for c in range(2):
                        def s_mm(kc):
                            j = kc % 2
                            S.op("pe", lambda pe: pe.matmul(PS[j][:], lhsT=kT[hb][:, c, kc * 128:(kc + 1) * 128], rhs=qT[hb][:, c, qt * TT:(qt + 1) * TT], start=True, stop=True),
                                 reads=[kTb[hb], qTb[hb]], writes=[PSb[j]])
                        s_mm(0)
                        for kc in range(NKC):
                            j = kc % 2
                            if kc + 1 < NKC:
                                s_mm(kc + 1)
                            if kc < NKC // 2:
                                tab, x0 = tabA, qt * TT - kc * 128 + CA
                            else:
                                tab, x0 = tabB, qt * TT - (kc - NKC // 2) * 128 + CA
                            S.op("dve", lambda v: v.scalar_tensor_tensor(stmp[j][:], tab[:, x0:x0 + TT], -slope / scale, PS[j][:], ALU.mult, ALU.add),
                                 reads=[PSb[j], tabb], writes=[stmpb[j]])
                            S.op("act", lambda a: a.activation(out=aT[j][:], in_=stmp[j][:], func=AF.Exp, scale=scale), reads=[stmpb[j]], writes=[aTb[j]])

                            def fn(pe):
                                ins = None
                                for qs in range(4):
                                    ins = pe.matmul(PS[2 + qs][:, 0:257], lhsT=aT[j][:, qs * 128:(qs + 1) * 128], rhs=vaug[hb][:, kc, 0:257],
                                                    start=(kc == 0), stop=(kc == NKC - 1))
                                return ins
                            if kc == 0:
                                S.op("pe", fn, reads=[aTb[j], vaugb[hb]], writes=[PSb[2], PSb[3], PSb[4], PSb[5]])
                            else:
                                S.op("pe", fn, reads=[aTb[j], vaugb[hb]], pwrites=[PSb[2], PSb[3], PSb[4], PSb[5]])
                        for qs in range(4):
                            S.op("dve", lambda v: v.reciprocal(small[:, qs:qs + 1], PS[2 + qs][:, 256:257]), reads=[PSb[2 + qs]], writes=[smallb])
                            S.op("dve", lambda v: v.tensor_scalar(U[:, c, qs, :], PS[2 + qs][:, 0:256], small[:, qs:qs + 1], None, ALU.mult),
                                 reads=[PSb[2 + qs], smallb], pwrites=[Ub])
                    ob_i = (h * NQT + qt) % 2
                    for qs in range(4):
                        S.op("dve", lambda v: v.scalar_tensor_tensor(dtmp[:], U[:, 1, qs, :], neglam[:, 0:1], U[:, 0, qs, :], ALU.mult, ALU.add),
                             reads=[Ub, lamb], writes=[dtb])
                        S.op("dve", lambda v: v.tensor_tensor(djunk[:], dtmp[:], dtmp[:], ALU.mult), reads=[dtb], pwrites=[dtb])
                        S.op("dve", lambda v: v.reduce_sum(small[:, 8:9], djunk[:], AX.X), reads=[dtb], writes=[smallb])
                        S.op("dve", lambda v: v.tensor_scalar(small[:, 9:10], small[:, 8:9], 1.0 / 256, 1e-5, ALU.mult, ALU.add), reads=[smallb], writes=[smallb])
                        S.op("act", lambda a: a.activation(out=small[:, 10:11], in_=small[:, 9:10], func=AF.Sqrt), reads=[smallb], writes=[smallb])
                        S.op("dve", lambda v: v.reciprocal(small[:, 11:12], small[:, 10:11]), reads=[smallb], writes=[smallb])
                        S.op("dve", lambda v: v.scalar_tensor_tensor(oab[:], dtmp[:], small[:, 11:12], gs_t[:], ALU.mult, ALU.mult),
                             reads=[dtb, smallb, lamb], pwrites=[dtb])

                        def fn(pe):
                            ins = None
                            for e in range(2):
                                ins = pe.transpose(PB[:, e * 128:(e + 1) * 128], oab[:, e * 128:(e + 1) * 128], ident_b[:])
                            return ins
                        S.op("pe", fn, reads=[dtb, cb], writes=[PBb])
                        S.op("act", lambda a: a.activation(out=oaTs[ob_i][:, :, qs * 128:(qs + 1) * 128], in_=PB[:, 0:256].rearrange("p (e f) -> p e f", f=128), func=AF.Copy),
                             reads=[PBb], pwrites=[oaTsb[ob_i]])
                    S.op("sp", lambda g: g.dma_start(out=OAT.ap()[h * 256:(h + 1) * 256, qt * TT:(qt + 1) * TT].rearrange("(e p) t -> p e t", p=128), in_=oaTs[ob_i][:]),
                         reads=[oaTsb[ob_i]], writes=[oatb[h][qt]], dkey="oaTs%d" % ob_i)

            ck(3)
            new_phase()
            stmp, stmpb, aT, aTb, small, smallb, t1, t1b = init_misc()
            iota_i = sb("iota_d", [128, 128], I32)
            DILS = (1, 4, 16)
            HMAX = 64 * 16
            dqT = [sb("dqT%d" % i, [128, T], BF16) for i in range(2)]
            dkl = [sb("dkl%d" % i, [128, T + 2 * HMAX], BF16) for i in range(2)]
            dvl = sb("dvl", [128, T + 2 * HMAX], BF16)
            dva = [sb("dva%d" % i, [128, 32, 144], BF16) for i in range(2)]
            dqTb = [Buf("dqT%d" % i) for i in range(2)]
            dklb = [Buf("dkl%d" % i) for i in range(2)]
            dvlb = Buf("dvl")
            dvab = [Buf("dva%d" % i) for i in range(2)]
            dab = sb("dab", [128, 256], F32)
            mab = sb("mab", [128, 256], F32)
            bab = sb("bab", [128, 256], F32)
            babb = Buf("bab")
            vb = sb("vb", [128, 2], F32)
            S.op("pool", lambda g: g.iota(iota_i[:, 0:128], pattern=[[-1, 128]], base=-64, channel_multiplier=1), writes=[babb])
            S.op("pool", lambda g: g.tensor_copy(out=dab[:, 0:128], in_=iota_i[:, 0:128]), reads=[babb], pwrites=[babb])
            S.op("dve", lambda v: v.tensor_scalar(dab[:, 128:256], dab[:, 0:128], 128.0, None, ALU.add), reads=[babb], pwrites=[babb])
            S.op("act", lambda a: a.activation(out=dab[:], in_=dab[:], func=AF.Abs), reads=[babb], writes=[babb])
            S.op("pool", lambda g: g.memset(mab[:], 0.0), pwrites=[babb])
            S.op("pool", lambda g: g.affine_select(out=mab[:, 0:128], in_=mab[:, 0:128], pattern=[[-1, 128]], compare_op=ALU.is_ge, fill=-1.0e6, base=0, channel_multiplier=1),
                 reads=[babb], writes=[babb])
            S.op("pool", lambda g: g.affine_select(out=mab[:, 128:256], in_=mab[:, 128:256], pattern=[[1, 128]], compare_op=ALU.is_ge, fill=-1.0e6, base=0, channel_multiplier=-1),
                 reads=[babb], writes=[babb])
            S.op("pool", lambda g: g.memset(vb[:], 0.0), reads=[babb], writes=[babb])
            S.op("dve", lambda v: v.tensor_copy(out=vb[0:64, 0:1], in_=meta[0:64, 1:2]), reads=[babb, cb], writes=[babb])
            S.op("dve", lambda v: v.tensor_copy(out=vb[64:128, 1:2], in_=meta[64:128, 2:3]), reads=[babb, cb], writes=[babb])
            for i in range(2):
                S.op("pool", lambda g: g.memset(dva[i][:, :, 128:130], 1.0), writes=[dvab[i]])
            ozs = [sb("ozs%d" % i, [128, 129], F32) for i in range(4)]
            ozsb = [Buf("ozs%d" % i) for i in range(4)]
            ozb = Buf("OZ")
            uk = 0
            gh = 0
            for g_i, dil in enumerate(DILS):
                H = 64 * dil
                Lq = T // dil
                nblk = Lq // 128
                nch = nblk + 1
                for h in range(8):
                    hb = gh % 2
                    gh += 1
                    slope = 2.0 ** (-8.0 * (h * 3 + g_i + 1) / 24)
                    ccq = C_LQ + g_i * 8 + h
                    cck = C_LK + g_i * 8 + h
                    ccv = C_LV + g_i * 8 + h
                    S.op("dve", lambda v: v.scalar_tensor_tensor(bab[:], dab[:], -slope * dil / scale, mab[:], ALU.mult, ALU.add), reads=[babb], writes=[babb])
                    S.op("sp", lambda q: q.dma_start(out=dqT[hb][:], in_=QKV.ap()[ccq * 128:(ccq + 1) * 128, 0:T]), reads=qkvb[ccq], writes=[dqTb[hb]], dkey="dqT%d" % hb)
                    for (dst, bufl, cc) in ((dkl[hb], dklb[hb], cck), (dvl, dvlb, ccv)):
                        S.op("sp", lambda q: q.dma_start(out=dst[:, 0:H], in_=QKV.ap()[cc * 128:(cc + 1) * 128, 2 * T - H:2 * T]), reads=qkvb[cc], writes=[bufl], dkey="dl_" + bufl.name)
                        S.op("sp", lambda q: q.dma_start(out=dst[:, H:H + T], in_=QKV.ap()[cc * 128:(cc + 1) * 128, 0:T]), reads=qkvb[cc], pwrites=[bufl], dkey="dl_" + bufl.name)
                        S.op("sp", lambda q: q.dma_start(out=dst[:, H + T:H + T + H], in_=QKV.ap()[cc * 128:(cc + 1) * 128, T:T + H]), reads=qkvb[cc], pwrites=[bufl], dkey="dl_" + bufl.name)

                    def kcol(r, c):
                        s = r + dil * (128 * c)
                        return slice(s, s + dil * 127 + 1, dil)
                    ids = [(r, c) for r in range(dil) for c in range(nch)]
                    for a0 in range(0, len(ids), 8):
                        grp = ids[a0:a0 + 8]

                        def fn(pe):
                            ins = None
                            for k2, (r, c) in enumerate(grp):
                                ins = pe.transpose(PB[:, k2 * 128:(k2 + 1) * 128], dvl[:, kcol(r, c)], ident_b[:])
                            return ins
                        S.op("pe", fn, reads=[dvlb, cb], writes=[PBb])
                        n8 = len(grp)
                        S.op("act", lambda a: a.activation(out=dva[hb][:, a0:a0 + n8, 0:128], in_=PB[:, 0:n8 * 128].rearrange("p (k f) -> p k f", f=128), func=AF.Copy),
                             reads=[PBb], pwrites=[dvab[hb]])
                    for r in range(dil):
                        for jb in range(nblk):
                            pi = uk % 2
                            oi = uk % 4
                            uk += 1
                            qs0 = r + dil * jb * 128
                            qsl = slice(qs0, qs0 + dil * 127 + 1, dil)

                            def fn(pe):
                                pe.matmul(PS[pi][:, 0:128], lhsT=dkl[hb][:, kcol(r, jb)], rhs=dqT[hb][:, qsl], start=True, stop=True)
                                return pe.matmul(PS[pi][:, 128:256], lhsT=dkl[hb][:, kcol(r, jb + 1)], rhs=dqT[hb][:, qsl], start=True, stop=True)
                            S.op("pe", fn, reads=[dklb[hb], dqTb[hb]], writes=[PSb[pi]])
                            S.op("dve", lambda v: v.tensor_tensor(stmp[pi][:, 0:256], PS[pi][:, 0:256], bab[:], ALU.add), reads=[PSb[pi], babb], writes=[stmpb[pi]])
                            biasA = vb[:, 0:1] if jb == 0 else 0.0
                            biasB = vb[:, 1:2] if jb == nblk - 1 else 0.0
                            S.op("act", lambda a: a.activation(out=aT[pi][:, 0:128], in_=stmp[pi][:, 0:128], func=AF.Exp, scale=scale, bias=biasA), reads=[stmpb[pi], babb], writes=[aTb[pi]])
                            S.op("act", lambda a: a.activation(out=aT[pi][:, 128:256], in_=stmp[pi][:, 128:256], func=AF.Exp, scale=scale, bias=biasB), reads=[stmpb[pi], babb], pwrites=[aTb[pi]])

                            def fn2(pe):
                                pe.matmul(PS[2 + pi][:, 0:129], lhsT=aT[pi][:, 0:128], rhs=dva[hb][:, r * nch + jb, 0:129], start=True, stop=False)
                                return pe.matmul(PS[2 + pi][:, 0:129], lhsT=aT[pi][:, 128:256], rhs=dva[hb][:, r * nch + jb + 1, 0:129], start=False, stop=True)
                            S.op("pe", fn2, reads=[aTb[pi], dvab[hb]], writes=[PSb[2 + pi]])
                            S.op("act", lambda a: a.activation(out=ozs[oi][:], in_=PS[2 + pi][:, 0:129], func=AF.Copy), reads=[PSb[2 + pi]], writes=[ozsb[oi]])
                            colo = (g_i * 8 + h) * 129
                            rsl = slice(qs0, qs0 + dil * 127 + 1, dil)
                            S.op("sp", lambda g: g.dma_start(out=OZ.ap()[rsl, colo:colo + 129], in_=ozs[oi][:]), reads=[ozsb[oi]], pwrites=[ozb], dkey="ozs%d" % oi)
            ozin = [sb("ozin%d" % i, [128, 24, 129], F32) for i in range(2)]
            ozinb = [Buf("ozin%d" % i) for i in range(2)]
            osum = sb("osum", [128, 8, 129], F32)
            osumb = Buf("osum")
            obb = sb("obb", [128, 8, 128], BF16)
            obTs = [sb("obTs%d" % i, [128, 8, 128], BF16) for i in range(2)]
            obTsb = [Buf("obTs%d" % i) for i in range(2)]
            obtb = [Buf("obt_%d" % t) for t in range(T // 128)]
            for tb in range(T // 128):
                i = tb % 2
                S.op("sp", lambda q: q.dma_start(out=ozin[i][:], in_=OZ.ap()[tb * 128:(tb + 1) * 128, :].rearrange("p (a f) -> p a f", f=129)), reads=[ozb], writes=[ozinb[i]], dkey="ozin%d" % i)
                S.op("dve", lambda v: v.tensor_tensor(osum[:], ozin[i][:, 0:8, :], ozin[i][:, 8:16, :], ALU.add), reads=[ozinb[i]], writes=[osumb])
                S.op("dve", lambda v: v.tensor_tensor(osum[:], osum[:], ozin[i][:, 16:24, :], ALU.add), reads=[ozinb[i]], writes=[osumb])
                S.op("dve", lambda v: v.reciprocal(small[:, 0:8], osum[:, :, 128]), reads=[osumb], writes=[smallb])
                for h in range(8):
                    S.op("dve", lambda v: v.tensor_scalar(obb[:, h, :], osum[:, h, 0:128], small[:, h:h + 1], None, ALU.mult), reads=[osumb, smallb], pwrites=[osumb])

                def fn(pe):
                    ins = None
                    for h in range(8):
                        ins = pe.transpose(PB[:, h * 128:(h + 1) * 128], obb[:, h, :], ident_b[:])
                    return ins
                S.op("pe", fn, reads=[osumb, cb], writes=[PBb])
                S.op("act", lambda a: a.activation(out=obTs[i][:], in_=PB[:].rearrange("p (h f) -> p h f", f=128), func=AF.Copy), reads=[PBb], writes=[obTsb[i]])
                S.op("sp", lambda g: g.dma_start(out=OBT.ap()[:, tb * 128:(tb + 1) * 128].rearrange("(h p) t -> p h t", p=128), in_=obTs[i][:]), reads=[obTsb[i]], writes=[obtb[tb]], dkey="obTs%d" % i)

            ck(4)
            new_phase()
            stmp, stmpb, aT, aTb, small, smallb, t1, t1b = init_misc()
            init_arena()
            init_ln()
            init_actin()
            oaT_t = sb("oaT_t", [128, 16, TT], BF16)
            obT_t = sb("obT_t", [128, 8, TT], BF16)
            oaT_tb = Buf("oaT_t")
            obT_tb = Buf("obT_t")
            yT = M["actT_in"]
            yTb = M["actT_inb"]
            sga = [sb("sga%d" % i, [128, TT], BF16) for i in range(2)]
            sgbb_t = [sb("sgbt%d" % i, [128, TT], BF16) for i in range(2)]
            sgab = [Buf("sga%d" % i) for i in range(2)]
            sgbb = [Buf("sgbt%d" % i) for i in range(2)]
            alloat = [b for h in range(8) for b in oatb[h]]
            pending = None
            for tt in range(NTT):
                S.op("sp", lambda q: q.dma_start(out=oaT_t[:], in_=OAT.ap()[:, tt * TT:(tt + 1) * TT].rearrange("(k p) t -> p k t", p=128)), reads=alloat, writes=[oaT_tb], dkey="oaT_t")
                S.op("sp", lambda q: q.dma_start(out=obT_t[:], in_=OBT.ap()[:, tt * TT:(tt + 1) * TT].rearrange("(k p) t -> p k t", p=128)), reads=obtb, writes=[obT_tb], dkey="obT_t")
                for dc in range(KD):
                    i = dc % 2
                    va, ba = load_wtile("wa", dc)
                    vbw, bbw = load_wtile("wb", dc)
                    S.op("pe", mm_group(PS[i][:], "wa", va, 0, W["wa"].KC, lambda kc: oaT_t[:, kc, :]), reads=ba + [oaT_tb], writes=[PSb[i]])
                    S.op("pe", mm_group(PS[2 + i][:], "wb", vbw, 0, W["wb"].KC, lambda kc: obT_t[:, kc, :]), reads=bbw + [obT_tb], writes=[PSb[2 + i]])
                    S.op("sp", lambda q: q.dma_start(out=sga[i][:], in_=QKV.ap()[(C_GA + dc) * 128:(C_GA + dc + 1) * 128, tt * TT:(tt + 1) * TT]), reads=[qkvb[C_GA + dc][tt]], writes=[sgab[i]], dkey="sga%d" % i)
                    S.op("sp", lambda q: q.dma_start(out=sgbb_t[i][:], in_=QKV.ap()[(C_GB + dc) * 128:(C_GB + dc + 1) * 128, tt * TT:(tt + 1) * TT]), reads=[qkvb[C_GB + dc][tt]], writes=[sgbb[i]], dkey="sgbt%d" % i)
                    S.op("dve", lambda v: v.tensor_tensor(t1[i][:], PS[i][:], sga[i][:], ALU.mult), reads=[PSb[i], sgab[i]], writes=[t1b[i]])
                    S.op("dve", lambda v: v.tensor_tensor(stmp[i][:], PS[2 + i][:], sgbb_t[i][:], ALU.mult), reads=[PSb[2 + i], sgbb[i]], writes=[stmpb[i]])
                    if dc == 0:
                        S.op("dve", lambda v: v.tensor_tensor(yT[:, dc, :], t1[i][:], stmp[i][:], ALU.add), reads=[t1b[i], stmpb[i]], writes=[yTb])
                    else:
                        S.op("dve", lambda v: v.tensor_tensor(yT[:, dc, :], t1[i][:], stmp[i][:], ALU.add), reads=[t1b[i], stmpb[i]], pwrites=[yTb])
                    ln_step(pending)
                ln_drain(pending)
                pending = None
                for dc in range(KD):
                    vm, bm = load_wtile("wmix", dc)
                    S.op("pe", mm_group(PS[4][:], "wmix", vm, 0, W["wmix"].KC, lambda kc: yT[:, kc, :]), reads=bm + [yTb], writes=[PSb[4]])
                    epilogue_resid(4, lambda: PS[4][:], [PSb[4]], HF[0], HFb[0], tt, dc)
                pending = ln_start(tt, 1, HF[1], HFb[1], HB[1], HBb[1])
            ln_drain(pending)

            ck(5)
            new_phase()
            ffn_phase("f2i", "f2o", HB[1], HBb[1], HF[1], HFb[1], 2, HF[2], HFb[2], HB[2], HBb[2], NTT)

            ck(6)
            new_phase()
            stmp, stmpb, aT, aTb, small, smallb, t1, t1b = init_misc()
            init_arena()
            init_ln()
            init_actin()
            pT_t = sb("pT_t", [128, 2, TT], BF16)
            pT_tb = Buf("pT_t")
            pending = None
            for tt in range(NTT):
                load_act_tile(HB[2], flat(HBb[2], tt), tt * TT)
                S.op("pool", lambda q: q.dma_start(out=pT_t[:], in_=pT_in.ap()[:, tt * TT:(tt + 1) * TT].rearrange("(k p) t -> p k t", p=128)), writes=[pT_tb], dkey="pT_t")
                for dc in range(KD):
                    i = dc % 2
                    vg, bg = load_wtile("wpg", dc)
                    vp, bp = load_wtile("wpp", dc)
                    S.op("pe", mm_group(PS[i][:], "wpg", vg, 0, W["wpg"].KC, lambda kc: M["actT_in"][:, kc, :]), reads=bg + [M["actT_inb"]], writes=[PSb[i]])
                    S.op("pe", mm_group(PS[2 + i][:], "wpp", vp, 0, W["wpp"].KC, lambda kc: pT_t[:, kc, :]), reads=bp + [pT_tb], writes=[PSb[2 + i]])
                    S.op("act", lambda a: a.activation(out=t1[i][:], in_=PS[i][:], func=AF.Sigmoid), reads=[PSb[i]], writes=[t1b[i]])
                    S.op("dve", lambda v: v.tensor_tensor(stmp[i][:], t1[i][:], PS[2 + i][:], ALU.mult), reads=[t1b[i], PSb[2 + i]], writes=[stmpb[i]])
                    epilogue_resid(None, lambda: stmp[i][:], [stmpb[i]], HF[2], HFb[2], tt, dc)
                    ln_step(pending)
                ln_drain(pending)
                pending = ln_start(tt, 3, out_T, outb, None, None)
            ln_drain(pending)
        except StopBuild:
            pass
        if cfg.get("dbg"):
            srcd = dict(h1=HF[0], h2=HF[1], h3=HF[2])[cfg["dbg"]]
            S.barrier()
            S.op("sp", lambda q: q.dma_start(out=out_T.ap(), in_=srcd.ap()[:, 0:T]), dkey="dbg")
        S.barrier()
        pstack[0].close()
    return nc


_CACHE = {}


def _host_inputs(cfg, inp):
    DM, T = cfg["DM"], cfg["T"]
    KD = DM // 128
    f32 = np.float32

    def lay(v):
        return np.ascontiguousarray(np.asarray(v, f32).reshape(KD, 128).T)
    lnp = np.concatenate([lay(inp[k][0]) for k in ("ln1_g", "ln1_b", "ln2_g", "ln2_b", "ln3_g", "ln3_b", "ln4_g", "ln4_b")], axis=1)
    lamv = np.concatenate([np.broadcast_to(np.asarray(inp[k][0], f32)[None, :], (128, 128)) for k in ("lam_q1", "lam_k1", "lam_q2", "lam_k2")], axis=1)
    subg = np.ascontiguousarray(np.broadcast_to(np.asarray(inp["subln_g"][0], f32)[None, :], (128, 256)))
    wsrc = dict(f1i="ffn1_w_in", f1o="ffn1_w_out", win="w_in", wa="w_branch_diff", wb="w_branch_dil", wmix="w_mix_out",
                f2i="ffn2_w_in", f2o="ffn2_w_out", wpg="w_ple_gate", wpp="w_ple_proj")
    wt = {}
    for n, src in wsrc.items():
        w = np.asarray(inp[src], f32)[0]
        K_, N_ = w.shape
        wt[n] = np.ascontiguousarray(w.reshape(K_ // 128, 128, N_ // 128, 128).transpose(2, 1, 0, 3)).reshape(N_ // 128, 128, K_)
    maps = []
    x = np.asarray(inp["x"], f32)
    p = np.asarray(inp["p"], f32)[0]
    for c in range(NCORES):
        b, half = c // 2, c % 2
        m = {}
        xl = np.concatenate([x[b, half * T:(half + 1) * T, :], x[b, (1 - half) * T:(2 - half) * T, :]], axis=0)
        m["xT"] = np.ascontiguousarray(xl.T)
        m["pT"] = np.ascontiguousarray(p[b, half * T:(half + 1) * T, :].T)
        meta = np.zeros((128, 16), f32)
        meta[:, 0] = (2 * half - 1) * T
        meta[:, 1] = 0.0 if half == 1 else -30000.0
        meta[:, 2] = 0.0 if half == 0 else -30000.0
        meta[:, 3 + (c ^ 1)] = 1.0
        m["meta"] = meta
        m["lnp"] = np.ascontiguousarray(lnp)
        m["lamv"] = np.ascontiguousarray(lamv)
        m["subg"] = subg
        for n in wsrc:
            m["w_" + n] = wt[n]
        maps.append(m)
    return maps


def run(cfg, inp, trace=False):
    key = (cfg["DM"], cfg["DFF"], cfg["T"])
    if key not in _CACHE:
        _CACHE[key] = build(cfg)
    nc = _CACHE[key]
    maps = _host_inputs(cfg, inp)
    res = run_bass_kernel_spmd(nc, maps, core_ids=list(range(NCORES)))
    T, DM = cfg["T"], cfg["DM"]
    out = np.empty((4, 2 * T, DM), np.float32)
    for c in range(NCORES):
        b, half = c // 2, c % 2
        out[b, half * T:(half + 1) * T, :] = res.results[c]["outT"].T
    return out


def kernel(**inputs):
    cfg = make_cfg()
    return run(cfg, inputs)
```

```python
import math
from contextlib import ExitStack
import numpy as np
import concourse.bass as bass
import concourse.mybir as mybir
from concourse.bass_utils import run_bass_kernel_spmd

F32 = mybir.dt.float32
BF16 = mybir.dt.bfloat16
I32 = mybir.dt.int32
AF = mybir.ActivationFunctionType
ALU = mybir.AluOpType
AX = mybir.AxisListType

NCORES = 8
TT = 512


def make_cfg(DM=4096, DFF=11008, T=2048):
    c = dict(DM=DM, DFF=DFF, T=T, PLE=256, NH=8, DH=128)
    c["QKW"] = 2048
    c["DVW"] = 2048
    c["DILW"] = 3072
    c["INW"] = 2 * 2048 + 2048 + 3 * 3072 + 2 * DM
    return c


class StopBuild(Exception):
    pass


class Buf:
    __slots__ = ("name", "w", "r")

    def __init__(self, name):
        self.name = name
        self.w = {}
        self.r = {}


class Sched:
    def __init__(self, nc, es):
        self.nc = nc
        self.es = es
        self.E = dict(pe=nc.tensor, act=nc.scalar, dve=nc.vector, pool=nc.gpsimd, sp=nc.sync)
        self.S = {}
        self.val = {}
        self.seen = {e: {} for e in self.E}
        for e in ("pe", "act", "dve", "pool"):
            self._mk(e)

    def _mk(self, key):
        self.S[key] = self.es.enter_context(self.nc.semaphore("s_" + key))
        self.val[key] = 0

    def op(self, e, fn, reads=(), writes=(), pwrites=(), dkey=None, inc=16):
        deps = {}

        def add(d):
            for k, v in d.items():
                if deps.get(k, 0) < v:
                    deps[k] = v

        for b in reads:
            add(b.w)
        for b in writes:
            add(b.w)
            add(b.r)
        for b in pwrites:
            add(b.r)
        eng = self.E[e]
        seen = self.seen[e]
        for k, v in deps.items():
            if k == e and e == "pe" and dkey is None:
                continue
            if seen.get(k, 0) >= v:
                continue
            eng.wait_ge(self.S[k], v)
            seen[k] = v
        ins = fn(eng)
        if dkey is not None:
            if dkey not in self.S:
                self._mk(dkey)
            self.val[dkey] += inc
            if inc == 1:
                ins.then_inc(self.S[dkey])
            else:
                ins.then_inc(self.S[dkey], inc)
            ev = (dkey, self.val[dkey])
        else:
            self.val[e] += 1
            ins.then_inc(self.S[e], 1)
            ev = (e, self.val[e])
        for b in reads:
            if b.r.get(ev[0], 0) < ev[1]:
                b.r[ev[0]] = ev[1]
        for b in writes:
            b.w = {ev[0]: ev[1]}
            b.r = {}
        for b in pwrites:
            if b.w.get(ev[0], 0) < ev[1]:
                b.w[ev[0]] = ev[1]
        return ev

    def barrier(self):
        for e, eng in self.E.items():
            for k, v in self.val.items():
                if v > 0 and self.seen[e].get(k, 0) < v:
                    eng.wait_ge(self.S[k], v)
                    self.seen[e][k] = v


class WSpec:
    def __init__(self, name, K, ncc):
        self.name = name
        self.K = K
        self.KC = K // 128
        self.ncc = ncc
        self.ksz = [128] * self.KC


def _groups(nchunks):
    out = []
    i = 0
    while i < len(nchunks):
        c0, w = nchunks[i]
        n = 1
        if w == 128:
            while i + n < len(nchunks) and nchunks[i + n] == (c0 + 128 * n, 128) and n < 32:
                n += 1
        out.append((i, c0, n, w))
        i += n
    return out


def build(cfg):
    DM, DFF, T = cfg["DM"], cfg["DFF"], cfg["T"]
    KD = DM // 128
    NTT = T // TT
    INW = cfg["INW"]
    alpha = 2.0 ** 0.25
    lambda_init = 0.8 - 0.6 * math.exp(-0.3 * 0)
    scale = 128.0 ** -0.5
    NF = DFF // 128
    fchunks = [(128 * i, 128) for i in range(NF)]
    NLT = 2 * NTT

    nc = bass.Bass("TRN2", target_bir_lowering=False)

    def din(name, shape, dt=F32):
        return nc.dram_tensor(name, list(shape), dt, kind="ExternalInput")

    def dscr(name, shape, dt=F32):
        return nc.dram_tensor(name, list(shape), dt)

    xT_in = din("xT", [DM, 2 * T])
    pT_in = din("pT", [256, T])
    meta_in = din("meta", [128, 16])
    lnp_in = din("lnp", [128, 8 * KD])
    lamv_in = din("lamv", [128, 4 * 128])
    subg_in = din("subg", [128, 256])
    out_T = nc.dram_tensor("outT", [DM, T], F32, kind="ExternalOutput")

    W = {}
    W["f1i"] = WSpec("f1i", DM, 2 * NF)
    W["f1o"] = WSpec("f1o", DFF, KD)
    W["win"] = WSpec("win", DM, INW // 128)
    W["wa"] = WSpec("wa", 2048, KD)
    W["wb"] = WSpec("wb", 1024, KD)
    W["wmix"] = WSpec("wmix", DM, KD)
    W["f2i"] = WSpec("f2i", DM, 2 * NF)
    W["f2o"] = WSpec("f2o", DFF, KD)
    W["wpg"] = WSpec("wpg", DM, KD)
    W["wpp"] = WSpec("wpp", 256, KD)
    worder = ["f1i", "f1o", "win", "wa", "wb", "wmix", "f2i", "f2o", "wpg", "wpp"]
    w_in_d = {}
    for n in worder:
        w_in_d[n] = din("w_" + n, [W[n].ncc, 128, W[n].KC * 128])

    RT = dscr("RT", [DM, 2 * T])
    HF = [dscr("HF%d" % i, [DM, 2 * T if i == 0 else T]) for i in range(3)]
    HB = [dscr("HB%d" % i, [DM, 2 * T if i == 0 else T], BF16) for i in range(3)]
    QKV = dscr("QKV", [INW, 2 * T], BF16)
    OAT = dscr("OAT", [2048, T], BF16)
    OBT = dscr("OBT", [1024, T], BF16)
    OZ = dscr("OZ", [T, 24 * 129])

    es = ExitStack()
    with es:
        S = Sched(nc, es)

        pstack = [ExitStack()]
        uniq = [0]

        def sbp(name, shape, dt):
            return es.enter_context(nc.sbuf_tensor("sb_" + name, list(shape), dt))

        def sb(name, shape, dt):
            uniq[0] += 1
            return pstack[0].enter_context(nc.sbuf_tensor("sb_%s_%d" % (name, uniq[0]), list(shape), dt))

        def new_phase():
            S.barrier()
            pstack[0].close()
            pstack[0] = ExitStack()

        PS = [es.enter_context(nc.psum_tensor("P%d" % i, [128, 512], F32)) for i in range(7)]
        PB = es.enter_context(nc.psum_tensor("PB", [128, 1024], BF16))
        PSb = [Buf("P%d" % i) for i in range(7)]
        PBb = Buf("PB")

        ident_f = sbp("ident_f", [128, 128], F32)
        ident_b = sbp("ident_b", [128, 128], BF16)
        ones_f = sbp("ones_f", [128, 128], F32)
        meta = sbp("meta", [128, 16], F32)
        lnp = sbp("lnp", [128, 8 * KD], F32)
        eps_t = sbp("eps_t", [128, 1], F32)
        cb = Buf("consts")
        S.op("pool", lambda g: g.memset(ident_f[:], 0.0), writes=[cb])
        S.op("pool", lambda g: g.affine_select(out=ident_f[:], in_=ident_f[:], pattern=[[-1, 128]],
                                               compare_op=ALU.not_equal, fill=1.0, base=0, channel_multiplier=1),
             writes=[cb])
        S.op("pool", lambda g: g.tensor_copy(out=ident_b[:], in_=ident_f[:]), reads=[cb], pwrites=[cb])
        S.op("pool", lambda g: g.memset(ones_f[:], 1.0), pwrites=[cb])
        S.op("pool", lambda g: g.memset(eps_t[:], 1e-5), pwrites=[cb])
        S.op("sp", lambda q: q.dma_start(out=meta[:], in_=meta_in[:, :]), pwrites=[cb], dkey="c_meta")
        S.op("sp", lambda q: q.dma_start(out=lnp[:], in_=lnp_in[:, :]), pwrites=[cb], dkey="c_lnp")

        NSLOT = 5
        M = {}
        slot_i = [0]

        def init_arena():
            M["arena"] = sb("arena", [128, NSLOT * 4096], BF16)
            M["slotb"] = [Buf("slot%d" % i) for i in range(NSLOT)]
            slot_i[0] = 0

        def load_wtile(n, cc, kc0=0, kc1=None):
            w = W[n]
            if kc1 is None:
                kc1 = w.KC
            nk = kc1 - kc0
            nsl = (nk * 128 + 4095) // 4096
            s0 = slot_i[0] % NSLOT
            if s0 + nsl > NSLOT:
                s0 = 0
            slot_i[0] = s0 + nsl
            arena = M["arena"]
            bufs = M["slotb"][s0:s0 + nsl]
            base = s0 * 4096
            dstv = arena[:, base:base + nk * 128]
            S.op("pool", lambda q: q.dma_start(out=dstv, in_=w_in_d[n].ap()[cc, :, kc0 * 128:kc1 * 128]), writes=bufs,
                 dkey="wl%d" % s0)

            def view(kc, ncols=128):
                k = kc - kc0
                return arena[:w.ksz[kc], base + k * 128: base + k * 128 + ncols]
            return view, bufs

        def mm_group(ps_ap, n, view, kc0, kc1, act, first=True, last=True):
            w = W[n]

            def fn(pe):
                ins = None
                for kc in range(kc0, kc1):
                    ins = pe.matmul(ps_ap, lhsT=view(kc), rhs=act(kc), start=(first and kc == kc0),
                                    stop=(last and kc == kc1 - 1))
                return ins
            return fn

        NR = 3
        L = {}
        cnt = dict(r=0, sq=0, z=0, res=0)
        P_SUM, P_SQ = 5, 6
        rtb = [[Buf("RT_%d_%d" % (t, d)) for d in range(KD)] for t in range(NLT)]

        def init_ln():
            L["rbuf"] = [sb("rbuf%d" % i, [128, TT], F32) for i in range(NR)]
            L["rbb"] = [Buf("rbuf%d" % i) for i in range(NR)]
            L["sqbuf"] = [sb("sqbuf%d" % i, [128, TT], F32) for i in range(2)]
            L["sqb"] = [Buf("sqbuf%d" % i) for i in range(2)]
            L["mean_t"] = sb("mean_t", [128, TT], F32)
            L["rstd_t"] = sb("rstd_t", [128, TT], F32)
            L["lntmp"] = sb("lntmp", [128, TT], F32)
            L["mrb"] = Buf("meanrstd")
            L["zin"] = [sb("zin%d" % i, [128, TT], F32) for i in range(2)]
            L["zinb"] = [Buf("zin%d" % i) for i in range(2)]
            L["hof"] = [sb("hof%d" % i, [128, TT], F32) for i in range(2)]
            L["hofb"] = [Buf("hof%d" % i) for i in range(2)]
            L["hob"] = [sb("hob%d" % i, [128, TT], BF16) for i in range(2)]
            L["hobb"] = [Buf("hob%d" % i) for i in range(2)]
            L["resid"] = [sb("resid%d" % i, [128, TT], F32) for i in range(2)]
            L["residb"] = [Buf("resid%d" % i) for i in range(2)]

        def init_actin():
            M["actT_in"] = sb("actT_in", [128, KD, TT], BF16)
            M["actT_inb"] = Buf("actT_in")

        def init_misc():
            M["stmp"] = [sb("stmp%d" % i, [128, TT], F32) for i in range(2)]
            M["stmpb"] = [Buf("stmp%d" % i) for i in range(2)]
            M["aT"] = [sb("aT%d" % i, [128, TT], BF16) for i in range(2)]
            M["aTb"] = [Buf("aT%d" % i) for i in range(2)]
            M["small"] = sb("small", [128, 16], F32)
            M["smallb"] = Buf("small")
            M["t1"] = [sb("t1_%d" % i, [128, TT], F32) for i in range(2)]
            M["t1b"] = [Buf("t1_%d" % i) for i in range(2)]
            return M["stmp"], M["stmpb"], M["aT"], M["aTb"], M["small"], M["smallb"], M["t1"], M["t1b"]

        def load_resid(src_dram, srcbufs, dc, tt):
            i = cnt["res"] % 2
            cnt["res"] += 1
            S.op("sp", lambda q: q.dma_start(out=L["resid"][i][:], in_=src_dram.ap()[dc * 128:(dc + 1) * 128, tt * TT:(tt + 1) * TT]),
                 reads=srcbufs, writes=[L["residb"][i]], dkey="resid%d" % i)
            return i

        def ln_piece(tt, dc, i):
            S.op("sp", lambda g: g.dma_start(out=RT.ap()[dc * 128:(dc + 1) * 128, tt * TT:(tt + 1) * TT], in_=L["rbuf"][i][:]),
                 reads=[L["rbb"][i]], writes=[rtb[tt][dc]], dkey="rst%d" % i)
            j = cnt["sq"] % 2
            cnt["sq"] += 1
            S.op("act", lambda a: a.activation(out=L["sqbuf"][j][:], in_=L["rbuf"][i][:], func=AF.Square), reads=[L["rbb"][i]], writes=[L["sqb"][j]])

            def fn(pe):
                pe.matmul(PS[P_SUM][:], lhsT=ones_f[:], rhs=L["rbuf"][i][:], start=(dc == 0), stop=(dc == KD - 1))
                return pe.matmul(PS[P_SQ][:], lhsT=ones_f[:], rhs=L["sqbuf"][j][:], start=(dc == 0), stop=(dc == KD - 1))
            if dc == 0:
                S.op("pe", fn, reads=[L["rbb"][i], L["sqb"][j], cb], writes=[PSb[P_SUM], PSb[P_SQ]])
            else:
                S.op("pe", fn, reads=[L["rbb"][i], L["sqb"][j], cb], pwrites=[PSb[P_SUM], PSb[P_SQ]])

        def ln_finish_gen(tt, gi, dstF, dstFb, dstB, dstBb):
            S.op("dve", lambda v: v.tensor_scalar(L["mean_t"][:], PS[P_SUM][:], 1.0 / DM, None, ALU.mult), reads=[PSb[P_SUM]], writes=[L["mrb"]])
            S.op("dve", lambda v: v.tensor_tensor(L["lntmp"][:], L["mean_t"][:], L["mean_t"][:], ALU.mult), reads=[L["mrb"]], pwrites=[L["mrb"]])
            S.op("dve", lambda v: v.scalar_tensor_tensor(L["rstd_t"][:], PS[P_SQ][:], 1.0 / DM, L["lntmp"][:], ALU.mult, ALU.subtract),
                 reads=[PSb[P_SQ], L["mrb"]], pwrites=[L["mrb"]])
            S.op("act", lambda a: a.activation(out=L["lntmp"][:], in_=L["rstd_t"][:], func=AF.Sqrt, bias=eps_t[:], scale=1.0), reads=[L["mrb"], cb], pwrites=[L["mrb"]])
            S.op("dve", lambda v: v.reciprocal(L["rstd_t"][:], L["lntmp"][:]), reads=[L["mrb"]], pwrites=[L["mrb"]])

            def issue_load(dc):
                i = cnt["z"] % 2
                cnt["z"] += 1
                sl = (slice(dc * 128, (dc + 1) * 128), slice(tt * TT, (tt + 1) * TT))
                S.op("sp", lambda q: q.dma_start(out=L["zin"][i][:], in_=RT.ap()[sl]), reads=[rtb[tt][dc]], writes=[L["zinb"][i]], dkey="zin%d" % i)
                return i
            nxt = issue_load(0)
            yield
            for dc in range(KD):
                i = nxt
                if dc + 1 < KD:
                    nxt = issue_load(dc + 1)
                sl = (slice(dc * 128, (dc + 1) * 128), slice(tt * TT, (tt + 1) * TT))
                S.op("dve", lambda v: v.tensor_tensor(L["zin"][i][:], L["zin"][i][:], L["mean_t"][:], ALU.subtract), reads=[L["mrb"]], writes=[L["zinb"][i]])
                S.op("dve", lambda v: v.tensor_tensor(L["zin"][i][:], L["zin"][i][:], L["rstd_t"][:], ALU.mult), reads=[L["mrb"]], writes=[L["zinb"][i]])
                g_ap = lnp[:, (2 * gi) * KD + dc:(2 * gi) * KD + dc + 1]
                b_ap = lnp[:, (2 * gi + 1) * KD + dc:(2 * gi + 1) * KD + dc + 1]
                S.op("act", lambda a: a.activation(out=L["hof"][i][:], in_=L["zin"][i][:], func=AF.Identity, bias=b_ap, scale=g_ap),
                     reads=[L["zinb"][i], cb], writes=[L["hofb"][i]])
                S.op("sp", lambda g: g.dma_start(out=dstF.ap()[sl], in_=L["hof"][i][:]), reads=[L["hofb"][i]], writes=[dstFb[tt][dc]], dkey="hof%d" % i)
                if dstB is not None:
                    S.op("act", lambda a: a.activation(out=L["hob"][i][:], in_=L["hof"][i][:], func=AF.Copy), reads=[L["hofb"][i]], writes=[L["hobb"][i]])
                    S.op("sp", lambda g: g.dma_start(out=dstB.ap()[sl], in_=L["hob"][i][:]), reads=[L["hobb"][i]], writes=[dstBb[tt][dc]], dkey="hob%d" % i)
                yield

        def ln_start(*args):
            g = ln_finish_gen(*args)
            next(g)
            return g

        def ln_step(g):
            if g is not None:
                next(g, None)

        def ln_drain(g):
            if g is not None:
                for _ in g:
                    pass

        def epilogue_resid(ps_i, src_ap_fn, srcbufs, resF, resFb, tt, dc):
            ri = load_resid(resF, [resFb[tt][dc]], dc, tt)
            i = cnt["r"] % NR
            cnt["r"] += 1
            S.op("dve", lambda v: v.scalar_tensor_tensor(L["rbuf"][i][:], L["resid"][ri][:], alpha, src_ap_fn(), ALU.mult, ALU.add),
                 reads=[L["residb"][ri]] + srcbufs, writes=[L["rbb"][i]])
            ln_piece(tt, dc, i)


        def load_act_tile(src_dram, srcbufs, col0, cast=False):
            S.op("pool" if cast else "sp", lambda q: q.dma_start(out=M["actT_in"][:], in_=src_dram.ap()[:, col0:col0 + TT].rearrange("(k p) t -> p k t", p=128)),
                 reads=srcbufs, writes=[M["actT_inb"]], dkey="actin")

        def dbufs(name):
            return [[Buf("%s_%d_%d" % (name, t, d)) for d in range(KD)] for t in range(NLT)]

        def flat(bb, tt):
            return bb[tt]

        xin_b = [[Buf("xin")] * KD for _ in range(NLT)]

        def ffn_phase(wi, wo, srcB, srcBb, resF, resFb, gi, dstF, dstFb, dstB, dstBb, ntiles, cast=False):
            wI, wO = W[wi], W[wo]
            init_arena()
            init_ln()
            init_actin()
            actT = sb("actT", [128, NF, TT], BF16)
            actTb = Buf("actT")
            sgbuf = [sb("sg%d" % i, [128, TT], F32) for i in range(2)]
            sgb = [Buf("sg%d" % i) for i in range(2)]
            it = 0
            pending = None
            step = max(1, NF // (KD + 1))
            load_act_tile(srcB, flat(srcBb, 0), 0, cast)
            for tt in range(ntiles):
                for f in range(NF):
                    fw = fchunks[f][1]
                    vg, bg = load_wtile(wi, f)
                    vu, bu = load_wtile(wi, NF + f)
                    pg, pu = (0, 1) if it % 2 == 0 else (2, 3)
                    it += 1
                    S.op("pe", mm_group(PS[pg][:fw, :], wi, lambda kc, v=vg: v(kc, fw), 0, wI.KC, lambda kc: M["actT_in"][:, kc, :]),
                         reads=bg + [M["actT_inb"]], writes=[PSb[pg]])
                    S.op("pe", mm_group(PS[pu][:fw, :], wi, lambda kc, v=vu: v(kc, fw), 0, wI.KC, lambda kc: M["actT_in"][:, kc, :]),
                         reads=bu + [M["actT_inb"]], writes=[PSb[pu]])
                    j = it % 2
                    S.op("act", lambda a: a.activation(out=sgbuf[j][:fw, :], in_=PS[pg][:fw, :], func=AF.Silu), reads=[PSb[pg]], writes=[sgb[j]])
                    S.op("dve", lambda v: v.scalar_tensor_tensor(actT[:fw, f, :], sgbuf[j][:fw, :], 0.5, PS[pu][:fw, :], ALU.mult, ALU.mult),
                         reads=[sgb[j], PSb[pu]], pwrites=[actTb])
                    if f % step == step - 1:
                        ln_step(pending)
                ln_drain(pending)
                pending = None
                if tt + 1 < ntiles:
                    load_act_tile(srcB, flat(srcBb, tt + 1), (tt + 1) * TT, cast)
                half = wO.KC // 2
                for dc in range(KD):
                    v1, b1 = load_wtile(wo, dc, 0, half)
                    v2, b2 = load_wtile(wo, dc, half, wO.KC)
                    S.op("pe", mm_group(PS[4][:], wo, v1, 0, half, lambda kc: actT[:wO.ksz[kc], kc, :], True, False),
                         reads=b1 + [actTb], writes=[PSb[4]])
                    S.op("pe", mm_group(PS[4][:], wo, v2, half, wO.KC, lambda kc: actT[:wO.ksz[kc], kc, :], False, True),
                         reads=b2 + [actTb], pwrites=[PSb[4]])
                    epilogue_resid(4, lambda: PS[4][:], [PSb[4]], resF, resFb, tt, dc)
                pending = ln_start(tt, gi, dstF, dstFb, dstB, dstBb)
            ln_drain(pending)

        HFb = [dbufs("HF%d" % i) for i in range(3)]
        HBb = [dbufs("HB%d" % i) for i in range(3)]
        outb = dbufs("out")

        def ck(k):
            if cfg.get('stop') == k:
                raise StopBuild()

        try:
            ck(0)
            ffn_phase("f1i", "f1o", xT_in, xin_b, xT_in, xin_b, 0, HF[0], HFb[0], HB[0], HBb[0], NLT, True)

            ck(1)
            new_phase()
            init_arena()
            init_actin()
            QKW, DILW = cfg["QKW"], cfg["DILW"]
            C_DQ, C_DK, C_DV = 0, QKW // 128, 2 * QKW // 128
            C_LQ = C_DV + 2048 // 128
            C_LK = C_LQ + DILW // 128
            C_LV = C_LK + DILW // 128
            C_GA = C_LV + DILW // 128
            C_GB = C_GA + KD
            NCC = C_GB + KD
            kv_ccs = list(range(C_DK, C_LQ)) + list(range(C_LK, C_GA))
            qkvb = [[Buf("qkv_%d_%d" % (cc, t)) for t in range(2 * NTT)] for cc in range(NCC)]
            ost = [sb("ost%d" % i, [128, TT], BF16) for i in range(2)]
            ostb = [Buf("ost%d" % i) for i in range(2)]
            k = 0
            for lt in range(2 * NTT):
                own = lt < NTT
                load_act_tile(HB[0], flat(HBb[0], lt), lt * TT)
                for cc in (range(NCC) if own else kv_ccs):
                    v, b = load_wtile("win", cc)
                    pi = k % 4
                    i = k % 2
                    k += 1
                    S.op("pe", mm_group(PS[pi][:], "win", v, 0, W["win"].KC, lambda kc: M["actT_in"][:, kc, :]), reads=b + [M["actT_inb"]], writes=[PSb[pi]])
                    if cc >= C_GA:
                        S.op("act", lambda a: a.activation(out=ost[i][:], in_=PS[pi][:], func=AF.Sigmoid), reads=[PSb[pi]], writes=[ostb[i]])
                    elif k % 2 == 0:
                        S.op("act", lambda a: a.activation(out=ost[i][:], in_=PS[pi][:], func=AF.Copy), reads=[PSb[pi]], writes=[ostb[i]])
                    else:
                        S.op("dve", lambda vv: vv.tensor_copy(out=ost[i][:], in_=PS[pi][:]), reads=[PSb[pi]], writes=[ostb[i]])
                    S.op("sp", lambda g: g.dma_start(out=QKV.ap()[cc * 128:(cc + 1) * 128, lt * TT:(lt + 1) * TT], in_=ost[i][:]),
                         reads=[ostb[i]], writes=[qkvb[cc][lt]], dkey="ost%d" % i)

            ck(2)
            new_phase()
            stmp, stmpb, aT, aTb, small, smallb, t1, t1b = init_misc()
            NKC = 2 * T // 128
            NQT = T // TT
            CA = T - 128
            TW = 2 * T - 128
            iota_i = sb("iota_i", [128, TW], I32)
            tabA = sb("tabA", [128, TW], F32)
            tabB = sb("tabB", [128, TW], F32)
            tabb = Buf("tabs")
            S.op("pool", lambda g: g.iota(iota_i[:], pattern=[[1, TW]], base=-CA, channel_multiplier=-1), writes=[tabb])
            S.op("pool", lambda g: g.tensor_copy(out=tabA[:], in_=iota_i[:]), reads=[tabb], pwrites=[tabb])
            S.op("act", lambda a: a.activation(out=tabB[:], in_=tabA[:], func=AF.Abs, bias=meta[:, 0:1], scale=1.0), reads=[tabb, cb], pwrites=[tabb])
            S.op("act", lambda a: a.activation(out=tabA[:], in_=tabA[:], func=AF.Abs), reads=[tabb], writes=[tabb])
            lamv = sb("lamv", [128, 512], F32)
            lamj = sb("lamj", [128, 128], F32)
            lams = sb("lams", [128, 4], F32)
            neglam = sb("neglam", [128, 1], F32)
            gs_t = sb("gs_t", [128, 256], F32)
            lamb = Buf("lam")
            S.op("sp", lambda q: q.dma_start(out=lamv[:], in_=lamv_in[:, :]), writes=[lamb], dkey="c_lam")
            S.op("sp", lambda q: q.dma_start(out=gs_t[:], in_=subg_in[:, :]), pwrites=[lamb], dkey="c_subg")
            S.op("dve", lambda v: v.memset(lams[:], 0.0), pwrites=[lamb])
            for j in range(2):
                S.op("dve", lambda v: v.tensor_tensor(lamj[:], lamv[:, (2 * j) * 128:(2 * j + 1) * 128], lamv[:, (2 * j + 1) * 128:(2 * j + 2) * 128], ALU.mult),
                     reads=[lamb], pwrites=[lamb])
                S.op("dve", lambda v: v.reduce_sum(lams[:, j:j + 1], lamj[:], AX.X), reads=[lamb], pwrites=[lamb])
            S.op("act", lambda a: a.activation(out=lams[:, 2:4], in_=lams[:, 0:2], func=AF.Exp), reads=[lamb], pwrites=[lamb])
            S.op("dve", lambda v: v.tensor_tensor(neglam[:], lams[:, 3:4], lams[:, 2:3], ALU.subtract), reads=[lamb], pwrites=[lamb])
            S.op("dve", lambda v: v.tensor_scalar(neglam[:], neglam[:], -lambda_init, None, ALU.add), reads=[lamb], pwrites=[lamb])
            S.op("dve", lambda v: v.tensor_scalar(gs_t[:], gs_t[:], 1.0 - lambda_init, None, ALU.mult), reads=[lamb], pwrites=[lamb])

            qT = [sb("qT%d" % i, [128, 2, T], BF16) for i in range(2)]
            kT = [sb("kT%d" % i, [128, 2, 2 * T], BF16) for i in range(2)]
            vT = sb("vT", [128, 2, 2 * T], BF16)
            vaug = [sb("vaug%d" % i, [128, NKC, 272], BF16) for i in range(2)]
            qTb = [Buf("qT%d" % i) for i in range(2)]
            kTb = [Buf("kT%d" % i) for i in range(2)]
            vTb = Buf("vT")
            vaugb = [Buf("vaug%d" % i) for i in range(2)]
            U = sb("U", [128, 2, 4, 256], F32)
            Ub = Buf("U")
            dtmp = sb("dtmp", [128, 256], F32)
            djunk = sb("djunk", [128, 256], F32)
            oab = sb("oab", [128, 256], BF16)
            dtb = Buf("dtmp")
            oaTs = [sb("oaTs%d" % i, [128, 2, TT], BF16) for i in range(2)]
            oaTsb = [Buf("oaTs%d" % i) for i in range(2)]
            oatb = [[Buf("oat_%d_%d" % (h, t)) for t in range(NQT)] for h in range(8)]
            for i in range(2):
                S.op("pool", lambda g: g.memset(vaug[i][:, :, 256:258], 1.0), writes=[vaugb[i]])

            def rows(cc0, n):
                return [b for cc in range(cc0, cc0 + n) for b in qkvb[cc]]

            for h in range(8):
                hb = h % 2
                slope = 2.0 ** (-8.0 * (h + 1) / 8)
                S.op("sp", lambda q: q.dma_start(out=qT[hb][:], in_=QKV.ap()[(C_DQ + 2 * h) * 128:(C_DQ + 2 * h + 2) * 128, 0:T].rearrange("(c p) t -> p c t", p=128)),
                     reads=rows(C_DQ + 2 * h, 2), writes=[qTb[hb]], dkey="qT%d" % hb)
                S.op("sp", lambda q: q.dma_start(out=kT[hb][:], in_=QKV.ap()[(C_DK + 2 * h) * 128:(C_DK + 2 * h + 2) * 128, :].rearrange("(c p) t -> p c t", p=128)),
                     reads=rows(C_DK + 2 * h, 2), writes=[kTb[hb]], dkey="kT%d" % hb)
                S.op("sp", lambda q: q.dma_start(out=vT[:], in_=QKV.ap()[(C_DV + 2 * h) * 128:(C_DV + 2 * h + 2) * 128, :].rearrange("(c p) t -> p c t", p=128)),
                     reads=rows(C_DV + 2 * h, 2), writes=[vTb], dkey="vT")
                for kc0 in range(0, NKC, 4):
                    def fn(pe):
                        ins = None
                        for kk in range(4):
                            for e in range(2):
                                ins = pe.transpose(PB[:, (kk * 2 + e) * 128:(kk * 2 + e + 1) * 128], vT[:, e, (kc0 + kk) * 128:(kc0 + kk + 1) * 128], ident_b[:])
                        return ins
                    S.op("pe", fn, reads=[vTb, cb], writes=[PBb])
                    S.op("act", lambda a: a.activation(out=vaug[hb][:, kc0:kc0 + 4, 0:256], in_=PB[:].rearrange("p (k f) -> p k f", f=256), func=AF.Copy),
                         reads=[PBb], pwrites=[vaugb[hb]])
                for qt in range(NQT):
                    for c in range(2):
                        SBK = (0, 1, 6)

                        def s_mm(kc):
                            jb = SBK[kc % 3]
                            S.op("pe", lambda pe: pe.matmul(PS[jb][:], lhsT=kT[hb][:, c, kc * 128:(kc + 1) * 128], rhs=qT[hb][:, c, qt * TT:(qt + 1) * TT], start=True, stop=True),
                                 reads=[kTb[hb], qTb[hb]], writes=[PSb[jb]])
                        s_mm(0)
                        s_mm(1)
                        for kc in range(NKC):
                            j = kc % 2
                            jb = SBK[kc % 3]
                            if kc + 2 < NKC:
                                s_mm(kc + 2)
                            if kc < NKC // 2:
                                tab, x0 = tabA, qt * TT - kc * 128 + CA
                            else:
                                tab, x0 = tabB, qt * TT - (kc - NKC // 2) * 128 + CA
                            S.op("dve", lambda v: v.scalar_tensor_tensor(stmp[j][:], tab[:, x0:x0 + TT], -slope / scale, PS[jb][:], ALU.mult, ALU.add),
                                 reads=[PSb[jb], tabb], writes=[stmpb[j]])
                            S.op("act", lambda a: a.activation(out=aT[j][:], in_=stmp[j][:], func=AF.Exp, scale=scale), reads=[stmpb[j]], writes=[aTb[j]])

                            def fn(pe):
                                ins = None
                                for qs in range(4):
                                    ins = pe.matmul(PS[2 + qs][:, 0:257], lhsT=aT[j][:, qs * 128:(qs + 1) * 128], rhs=vaug[hb][:, kc, 0:257],
                                                    start=(kc == 0), stop=(kc == NKC - 1))
                                return ins
                            if kc == 0:
                                S.op("pe", fn, reads=[aTb[j], vaugb[hb]], writes=[PSb[2], PSb[3], PSb[4], PSb[5]])
                            else:
                                S.op("pe", fn, reads=[aTb[j], vaugb[hb]], pwrites=[PSb[2], PSb[3], PSb[4], PSb[5]])
                        for qs in range(4):
                            S.op("dve", lambda v: v.reciprocal(small[:, qs:qs + 1], PS[2 + qs][:, 256:257]), reads=[PSb[2 + qs]], writes=[smallb])
                            S.op("dve", lambda v: v.tensor_scalar(U[:, c, qs, :], PS[2 + qs][:, 0:256], small[:, qs:qs + 1], None, ALU.mult),
                                 reads=[PSb[2 + qs], smallb], pwrites=[Ub])
                    ob_i = (h * NQT + qt) % 2
                    for qs in range(4):
                        S.op("dve", lambda v: v.scalar_tensor_tensor(dtmp[:], U[:, 1, qs, :], neglam[:, 0:1], U[:, 0, qs, :], ALU.mult, ALU.add),
                             reads=[Ub, lamb], writes=[dtb])
                        S.op("dve", lambda v: v.tensor_tensor(djunk[:], dtmp[:], dtmp[:], ALU.mult), reads=[dtb], pwrites=[dtb])
                        S.op("dve", lambda v: v.reduce_sum(small[:, 8:9], djunk[:], AX.X), reads=[dtb], writes=[smallb])
                        S.op("dve", lambda v: v.tensor_scalar(small[:, 9:10], small[:, 8:9], 1.0 / 256, 1e-5, ALU.mult, ALU.add), reads=[smallb], writes=[smallb])
                        S.op("act", lambda a: a.activation(out=small[:, 10:11], in_=small[:, 9:10], func=AF.Sqrt), reads=[smallb], writes=[smallb])
                        S.op("dve", lambda v: v.reciprocal(small[:, 11:12], small[:, 10:11]), reads=[smallb], writes=[smallb])
                        S.op("dve", lambda v: v.scalar_tensor_tensor(oab[:], dtmp[:], small[:, 11:12], gs_t[:], ALU.mult, ALU.mult),
                             reads=[dtb, smallb, lamb], pwrites=[dtb])

                        def fn(pe):
                            ins = None
                            for e in range(2):
                                ins = pe.transpose(PB[:, e * 128:(e + 1) * 128], oab[:, e * 128:(e + 1) * 128], ident_b[:])
                            return ins
                        S.op("pe", fn, reads=[dtb, cb], writes=[PBb])
                        S.op("act", lambda a: a.activation(out=oaTs[ob_i][:, :, qs * 128:(qs + 1) * 128], in_=PB[:, 0:256].rearrange("p (e f) -> p e f", f=128), func=AF.Copy),
                             reads=[PBb], pwrites=[oaTsb[ob_i]])
                    S.op("sp", lambda g: g.dma_start(out=OAT.ap()[h * 256:(h + 1) * 256, qt * TT:(qt + 1) * TT].rearrange("(e p) t -> p e t", p=128), in_=oaTs[ob_i][:]),
                         reads=[oaTsb[ob_i]], writes=[oatb[h][qt]], dkey="oaTs%d" % ob_i)

            ck(3)
            new_phase()
            stmp, stmpb, aT, aTb, small, smallb, t1, t1b = init_misc()
            iota_i = sb("iota_d", [128, 128], I32)
            DILS = (1, 4, 16)
            HMAX = 64 * 16
            dqT = [sb("dqT%d" % i, [128, T], BF16) for i in range(2)]
            dkl = [sb("dkl%d" % i, [128, T + 2 * HMAX], BF16) for i in range(2)]
            dvl = sb("dvl", [128, T + 2 * HMAX], BF16)
            dva = [sb("dva%d" % i, [128, 32, 144], BF16) for i in range(2)]
            dqTb = [Buf("dqT%d" % i) for i in range(2)]
            dklb = [Buf("dkl%d" % i) for i in range(2)]
            dvlb = Buf("dvl")
            dvab = [Buf("dva%d" % i) for i in range(2)]
            dab = sb("dab", [128, 256], F32)
            mab = sb("mab", [128, 256], F32)
            bab = sb("bab", [128, 256], F32)
            babb = Buf("bab")
            vb = sb("vb", [128, 2], F32)
            S.op("pool", lambda g: g.iota(iota_i[:, 0:128], pattern=[[-1, 128]], base=-64, channel_multiplier=1), writes=[babb])
            S.op("pool", lambda g: g.tensor_copy(out=dab[:, 0:128], in_=iota_i[:, 0:128]), reads=[babb], pwrites=[babb])
            S.op("dve", lambda v: v.tensor_scalar(dab[:, 128:256], dab[:, 0:128], 128.0, None, ALU.add), reads=[babb], pwrites=[babb])
            S.op("act", lambda a: a.activation(out=dab[:], in_=dab[:], func=AF.Abs), reads=[babb], writes=[babb])
            S.op("pool", lambda g: g.memset(mab[:], 0.0), pwrites=[babb])
            S.op("pool", lambda g: g.affine_select(out=mab[:, 0:128], in_=mab[:, 0:128], pattern=[[-1, 128]], compare_op=ALU.is_ge, fill=-1.0e6, base=0, channel_multiplier=1),
                 reads=[babb], writes=[babb])
            S.op("pool", lambda g: g.affine_select(out=mab[:, 128:256], in_=mab[:, 128:256], pattern=[[1, 128]], compare_op=ALU.is_ge, fill=-1.0e6, base=0, channel_multiplier=-1),
                 reads=[babb], writes=[babb])
            S.op("pool", lambda g: g.memset(vb[:], 0.0), reads=[babb], writes=[babb])
            S.op("dve", lambda v: v.tensor_copy(out=vb[0:64, 0:1], in_=meta[0:64, 1:2]), reads=[babb, cb], writes=[babb])
            S.op("dve", lambda v: v.tensor_copy(out=vb[64:128, 1:2], in_=meta[64:128, 2:3]), reads=[babb, cb], writes=[babb])
            for i in range(2):
                S.op("pool", lambda g: g.memset(dva[i][:, :, 128:130], 1.0), writes=[dvab[i]])
            ozs = [sb("ozs%d" % i, [128, 129], F32) for i in range(4)]
            ozsb = [Buf("ozs%d" % i) for i in range(4)]
            ozb = Buf("OZ")
            uk = 0
            gh = 0
            for g_i, dil in enumerate(DILS):
                H = 64 * dil
                Lq = T // dil
                nblk = Lq // 128
                nch = nblk + 1
                for h in range(8):
                    hb = gh % 2
                    gh += 1
                    slope = 2.0 ** (-8.0 * (h * 3 + g_i + 1) / 24)
                    ccq = C_LQ + g_i * 8 + h
                    cck = C_LK + g_i * 8 + h
                    ccv = C_LV + g_i * 8 + h
                    S.op("dve", lambda v: v.scalar_tensor_tensor(bab[:], dab[:], -slope * dil / scale, mab[:], ALU.mult, ALU.add), reads=[babb], writes=[babb])
                    S.op("sp", lambda q: q.dma_start(out=dqT[hb][:], in_=QKV.ap()[ccq * 128:(ccq + 1) * 128, 0:T]), reads=qkvb[ccq], writes=[dqTb[hb]], dkey="dqT%d" % hb)
                    for (dst, bufl, cc) in ((dkl[hb], dklb[hb], cck), (dvl, dvlb, ccv)):
                        S.op("sp", lambda q: q.dma_start(out=dst[:, 0:H], in_=QKV.ap()[cc * 128:(cc + 1) * 128, 2 * T - H:2 * T]), reads=qkvb[cc], writes=[bufl], dkey="dl_" + bufl.name)
                        S.op("sp", lambda q: q.dma_start(out=dst[:, H:H + T], in_=QKV.ap()[cc * 128:(cc + 1) * 128, 0:T]), reads=qkvb[cc], pwrites=[bufl], dkey="dl_" + bufl.name)
                        S.op("sp", lambda q: q.dma_start(out=dst[:, H + T:H + T + H], in_=QKV.ap()[cc * 128:(cc + 1) * 128, T:T + H]), reads=qkvb[cc], pwrites=[bufl], dkey="dl_" + bufl.name)

                    def kcol(r, c):
                        s = r + dil * (128 * c)
                        return slice(s, s + dil * 127 + 1, dil)
                    ids = [(r, c) for r in range(dil) for c in range(nch)]
                    for a0 in range(0, len(ids), 8):
                        grp = ids[a0:a0 + 8]

                        def fn(pe):
                            ins = None
                            for k2, (r, c) in enumerate(grp):
                                ins = pe.transpose(PB[:, k2 * 128:(k2 + 1) * 128], dvl[:, kcol(r, c)], ident_b[:])
                            return ins
                        S.op("pe", fn, reads=[dvlb, cb], writes=[PBb])
                        n8 = len(grp)
                        S.op("act", lambda a: a.activation(out=dva[hb][:, a0:a0 + n8, 0:128], in_=PB[:, 0:n8 * 128].rearrange("p (k f) -> p k f", f=128), func=AF.Copy),
                             reads=[PBb], pwrites=[dvab[hb]])
                    for r in range(dil):
                        for jb in range(nblk):
                            pi = uk % 2
                            oi = uk % 4
                            uk += 1
                            qs0 = r + dil * jb * 128
                            qsl = slice(qs0, qs0 + dil * 127 + 1, dil)

                            def fn(pe):
                                pe.matmul(PS[pi][:, 0:128], lhsT=dkl[hb][:, kcol(r, jb)], rhs=dqT[hb][:, qsl], start=True, stop=True)
                                return pe.matmul(PS[pi][:, 128:256], lhsT=dkl[hb][:, kcol(r, jb + 1)], rhs=dqT[hb][:, qsl], start=True, stop=True)
                            S.op("pe", fn, reads=[dklb[hb], dqTb[hb]], writes=[PSb[pi]])
                            S.op("dve", lambda v: v.tensor_tensor(stmp[pi][:, 0:256], PS[pi][:, 0:256], bab[:], ALU.add), reads=[PSb[pi], babb], writes=[stmpb[pi]])
                            biasA = vb[:, 0:1] if jb == 0 else 0.0
                            biasB = vb[:, 1:2] if jb == nblk - 1 else 0.0
                            S.op("act", lambda a: a.activation(out=aT[pi][:, 0:128], in_=stmp[pi][:, 0:128], func=AF.Exp, scale=scale, bias=biasA), reads=[stmpb[pi], babb], writes=[aTb[pi]])
                            S.op("act", lambda a: a.activation(out=aT[pi][:, 128:256], in_=stmp[pi][:, 128:256], func=AF.Exp, scale=scale, bias=biasB), reads=[stmpb[pi], babb], pwrites=[aTb[pi]])

                            def fn2(pe):
                                pe.matmul(PS[2 + pi][:, 0:129], lhsT=aT[pi][:, 0:128], rhs=dva[hb][:, r * nch + jb, 0:129], start=True, stop=False)
                                return pe.matmul(PS[2 + pi][:, 0:129], lhsT=aT[pi][:, 128:256], rhs=dva[hb][:, r * nch + jb + 1, 0:129], start=False, stop=True)
                            S.op("pe", fn2, reads=[aTb[pi], dvab[hb]], writes=[PSb[2 + pi]])
                            S.op("act", lambda a: a.activation(out=ozs[oi][:], in_=PS[2 + pi][:, 0:129], func=AF.Copy), reads=[PSb[2 + pi]], writes=[ozsb[oi]])
                            colo = (g_i * 8 + h) * 129
                            rsl = slice(qs0, qs0 + dil * 127 + 1, dil)
                            S.op("sp", lambda g: g.dma_start(out=OZ.ap()[rsl, colo:colo + 129], in_=ozs[oi][:]), reads=[ozsb[oi]], pwrites=[ozb], dkey="ozs%d" % oi)
            ozin = [sb("ozin%d" % i, [128, 24, 129], F32) for i in range(2)]
            ozinb = [Buf("ozin%d" % i) for i in range(2)]
            osum = sb("osum", [128, 8, 129], F32)
            osumb = Buf("osum")
            obb = sb("obb", [128, 8, 128], BF16)
            obTs = [sb("obTs%d" % i, [128, 8, 128], BF16) for i in range(2)]
            obTsb = [Buf("obTs%d" % i) for i in range(2)]
            obtb = [Buf("obt_%d" % t) for t in range(T // 128)]
            for tb in range(T // 128):
                i = tb % 2
                S.op("sp", lambda q: q.dma_start(out=ozin[i][:], in_=OZ.ap()[tb * 128:(tb + 1) * 128, :].rearrange("p (a f) -> p a f", f=129)), reads=[ozb], writes=[ozinb[i]], dkey="ozin%d" % i)
                S.op("dve", lambda v: v.tensor_tensor(osum[:], ozin[i][:, 0:8, :], ozin[i][:, 8:16, :], ALU.add), reads=[ozinb[i]], writes=[osumb])
                S.op("dve", lambda v: v.tensor_tensor(osum[:], osum[:], ozin[i][:, 16:24, :], ALU.add), reads=[ozinb[i]], writes=[osumb])
                S.op("dve", lambda v: v.reciprocal(small[:, 0:8], osum[:, :, 128]), reads=[osumb], writes=[smallb])
                for h in range(8):
                    S.op("dve", lambda v: v.tensor_scalar(obb[:, h, :], osum[:, h, 0:128], small[:, h:h + 1], None, ALU.mult), reads=[osumb, smallb], pwrites=[osumb])

                def fn(pe):
                    ins = None
                    for h in range(8):
                        ins = pe.transpose(PB[:, h * 128:(h + 1) * 128], obb[:, h, :], ident_b[:])
                    return ins
                S.op("pe", fn, reads=[osumb, cb], writes=[PBb])
                S.op("act", lambda a: a.activation(out=obTs[i][:], in_=PB[:].rearrange("p (h f) -> p h f", f=128), func=AF.Copy), reads=[PBb], writes=[obTsb[i]])
                S.op("sp", lambda g: g.dma_start(out=OBT.ap()[:, tb * 128:(tb + 1) * 128].rearrange("(h p) t -> p h t", p=128), in_=obTs[i][:]), reads=[obTsb[i]], writes=[obtb[tb]], dkey="obTs%d" % i)

            ck(4)
            new_phase()
            stmp, stmpb, aT, aTb, small, smallb, t1, t1b = init_misc()
            init_arena()
            init_ln()
            init_actin()
            oaT_t = sb("oaT_t", [128, 16, TT], BF16)
            obT_t = sb("obT_t", [128, 8, TT], BF16)
            oaT_tb = Buf("oaT_t")
            obT_tb = Buf("obT_t")
            yT = M["actT_in"]
            yTb = M["actT_inb"]
            sga = [sb("sga%d" % i, [128, TT], BF16) for i in range(2)]
            sgbb_t = [sb("sgbt%d" % i, [128, TT], BF16) for i in range(2)]
            sgab = [Buf("sga%d" % i) for i in range(2)]
            sgbb = [Buf("sgbt%d" % i) for i in range(2)]
            alloat = [b for h in range(8) for b in oatb[h]]
            pending = None
            for tt in range(NTT):
                S.op("sp", lambda q: q.dma_start(out=oaT_t[:], in_=OAT.ap()[:, tt * TT:(tt + 1) * TT].rearrange("(k p) t -> p k t", p=128)), reads=alloat, writes=[oaT_tb], dkey="oaT_t")
                S.op("sp", lambda q: q.dma_start(out=obT_t[:], in_=OBT.ap()[:, tt * TT:(tt + 1) * TT].rearrange("(k p) t -> p k t", p=128)), reads=obtb, writes=[obT_tb], dkey="obT_t")
                for dc in range(KD):
                    i = dc % 2
                    va, ba = load_wtile("wa", dc)
                    vbw, bbw = load_wtile("wb", dc)
                    S.op("pe", mm_group(PS[i][:], "wa", va, 0, W["wa"].KC, lambda kc: oaT_t[:, kc, :]), reads=ba + [oaT_tb], writes=[PSb[i]])
                    S.op("pe", mm_group(PS[2 + i][:], "wb", vbw, 0, W["wb"].KC, lambda kc: obT_t[:, kc, :]), reads=bbw + [obT_tb], writes=[PSb[2 + i]])
                    S.op("sp", lambda q: q.dma_start(out=sga[i][:], in_=QKV.ap()[(C_GA + dc) * 128:(C_GA + dc + 1) * 128, tt * TT:(tt + 1) * TT]), reads=[qkvb[C_GA + dc][tt]], writes=[sgab[i]], dkey="sga%d" % i)
                    S.op("sp", lambda q: q.dma_start(out=sgbb_t[i][:], in_=QKV.ap()[(C_GB + dc) * 128:(C_GB + dc + 1) * 128, tt * TT:(tt + 1) * TT]), reads=[qkvb[C_GB + dc][tt]], writes=[sgbb[i]], dkey="sgbt%d" % i)
                    S.op("dve", lambda v: v.tensor_tensor(t1[i][:], PS[i][:], sga[i][:], ALU.mult), reads=[PSb[i], sgab[i]], writes=[t1b[i]])
                    S.op("dve", lambda v: v.tensor_tensor(stmp[i][:], PS[2 + i][:], sgbb_t[i][:], ALU.mult), reads=[PSb[2 + i], sgbb[i]], writes=[stmpb[i]])
                    if dc == 0:
                        S.op("dve", lambda v: v.tensor_tensor(yT[:, dc, :], t1[i][:], stmp[i][:], ALU.add), reads=[t1b[i], stmpb[i]], writes=[yTb])
                    else:
                        S.op("dve", lambda v: v.tensor_tensor(yT[:, dc, :], t1[i][:], stmp[i][:], ALU.add), reads=[t1b[i], stmpb[i]], pwrites=[yTb])
                    ln_step(pending)
                ln_drain(pending)
                pending = None
                for dc in range(KD):
                    vm, bm = load_wtile("wmix", dc)
                    S.op("pe", mm_group(PS[4][:], "wmix", vm, 0, W["wmix"].KC, lambda kc: yT[:, kc, :]), reads=bm + [yTb], writes=[PSb[4]])
                    epilogue_resid(4, lambda: PS[4][:], [PSb[4]], HF[0], HFb[0], tt, dc)
                pending = ln_start(tt, 1, HF[1], HFb[1], HB[1], HBb[1])
            ln_drain(pending)

            ck(5)
            new_phase()
            ffn_phase("f2i", "f2o", HB[1], HBb[1], HF[1], HFb[1], 2, HF[2], HFb[2], HB[2], HBb[2], NTT)

            ck(6)
            new_phase()
            stmp, stmpb, aT, aTb, small, smallb, t1, t1b = init_misc()
            init_arena()
            init_ln()
            init_actin()
            pT_t = sb("pT_t", [128, 2, TT], BF16)
            pT_tb = Buf("pT_t")
            pending = None
            for tt in range(NTT):
                load_act_tile(HB[2], flat(HBb[2], tt), tt * TT)
                S.op("pool", lambda q: q.dma_start(out=pT_t[:], in_=pT_in.ap()[:, tt * TT:(tt + 1) * TT].rearrange("(k p) t -> p k t", p=128)), writes=[pT_tb], dkey="pT_t")
                for dc in range(KD):
                    i = dc % 2
                    vg, bg = load_wtile("wpg", dc)
                    vp, bp = load_wtile("wpp", dc)
                    S.op("pe", mm_group(PS[i][:], "wpg", vg, 0, W["wpg"].KC, lambda kc: M["actT_in"][:, kc, :]), reads=bg + [M["actT_inb"]], writes=[PSb[i]])
                    S.op("pe", mm_group(PS[2 + i][:], "wpp", vp, 0, W["wpp"].KC, lambda kc: pT_t[:, kc, :]), reads=bp + [pT_tb], writes=[PSb[2 + i]])
                    S.op("act", lambda a: a.activation(out=t1[i][:], in_=PS[i][:], func=AF.Sigmoid), reads=[PSb[i]], writes=[t1b[i]])
                    S.op("dve", lambda v: v.tensor_tensor(stmp[i][:], t1[i][:], PS[2 + i][:], ALU.mult), reads=[t1b[i], PSb[2 + i]], writes=[stmpb[i]])
                    epilogue_resid(None, lambda: stmp[i][:], [stmpb[i]], HF[2], HFb[2], tt, dc)
                    ln_step(pending)
                ln_drain(pending)
                pending = ln_start(tt, 3, out_T, outb, None, None)
            ln_drain(pending)
        except StopBuild:
            pass
        if cfg.get("dbg"):
            srcd = dict(h1=HF[0], h2=HF[1], h3=HF[2])[cfg["dbg"]]
            S.barrier()
            S.op("sp", lambda q: q.dma_start(out=out_T.ap(), in_=srcd.ap()[:, 0:T]), dkey="dbg")
        S.barrier()
        pstack[0].close()
    return nc


_CACHE = {}


def _host_inputs(cfg, inp):
    DM, T = cfg["DM"], cfg["T"]
    KD = DM // 128
    f32 = np.float32

    def lay(v):
        return np.ascontiguousarray(np.asarray(v, f32).reshape(KD, 128).T)
    lnp = np.concatenate([lay(inp[k][0]) for k in ("ln1_g", "ln1_b", "ln2_g", "ln2_b", "ln3_g", "ln3_b", "ln4_g", "ln4_b")], axis=1)
    lamv = np.concatenate([np.broadcast_to(np.asarray(inp[k][0], f32)[None, :], (128, 128)) for k in ("lam_q1", "lam_k1", "lam_q2", "lam_k2")], axis=1)
    subg = np.ascontiguousarray(np.broadcast_to(np.asarray(inp["subln_g"][0], f32)[None, :], (128, 256)))
    wsrc = dict(f1i="ffn1_w_in", f1o="ffn1_w_out", win="w_in", wa="w_branch_diff", wb="w_branch_dil", wmix="w_mix_out",
                f2i="ffn2_w_in", f2o="ffn2_w_out", wpg="w_ple_gate", wpp="w_ple_proj")
    wt = {}
    for n, src in wsrc.items():
        w = np.asarray(inp[src], f32)[0]
        K_, N_ = w.shape
        wt[n] = np.ascontiguousarray(w.reshape(K_ // 128, 128, N_ // 128, 128).transpose(2, 1, 0, 3)).reshape(N_ // 128, 128, K_)
    maps = []
    x = np.asarray(inp["x"], f32)
    p = np.asarray(inp["p"], f32)[0]
    for c in range(NCORES):
        b, half = c // 2, c % 2
        m = {}
        xl = np.concatenate([x[b, half * T:(half + 1) * T, :], x[b, (1 - half) * T:(2 - half) * T, :]], axis=0)
        m["xT"] = np.ascontiguousarray(xl.T)
        m["pT"] = np.ascontiguousarray(p[b, half * T:(half + 1) * T, :].T)
        meta = np.zeros((128, 16), f32)
        meta[:, 0] = (2 * half - 1) * T
        meta[:, 1] = 0.0 if half == 1 else -30000.0
        meta[:, 2] = 0.0 if half == 0 else -30000.0
        meta[:, 3 + (c ^ 1)] = 1.0
        m["meta"] = meta
        m["lnp"] = np.ascontiguousarray(lnp)
        m["lamv"] = np.ascontiguousarray(lamv)
        m["subg"] = subg
        for n in wsrc:
            m["w_" + n] = wt[n]
        maps.append(m)
    return maps


def run(cfg, inp, trace=False):
    key = (cfg["DM"], cfg["DFF"], cfg["T"])
    if key not in _CACHE:
        _CACHE[key] = build(cfg)
    nc = _CACHE[key]
    maps = _host_inputs(cfg, inp)
    res = run_bass_kernel_spmd(nc, maps, core_ids=list(range(NCORES)))
    T, DM = cfg["T"], cfg["DM"]
    out = np.empty((4, 2 * T, DM), np.float32)
    for c in range(NCORES):
        b, half = c // 2, c % 2
        out[b, half * T:(half + 1) * T, :] = res.results[c]["outT"].T
    return out


def kernel(**inputs):
    cfg = make_cfg()
    return run(cfg, inputs)
```
